# Optimizing a Trainium2 kernel written in Bass

```python
import jax, jax.numpy as jnp
from jax import lax
import numpy as np

D_MODEL = 1024
BATCH = 32
SEQ = 256
DEPTH = 2
DEC_BATCH = 4
DEC_SEQ = 1024
PAST_LEN = 256

GRID_W = 64
MLSTM_H = 4
MLSTM_DH = 256
MLSTM_W = MLSTM_H * MLSTM_DH
CHUNK = 128
FOURIER_G = 4
FOURIER_GW = 256
FOURIER_W = FOURIER_G * FOURIER_GW
CONV_W = 1024
N_BRANCH = 3
D_FF = -(-8 * D_MODEL // (3 * 256)) * 256
ALPHA = (2 * DEPTH) ** 0.25
BETA = (8 * DEPTH) ** -0.25
LN_EPS = 1e-5
POS_BASE = 10000.0
IN_SPLITS = (MLSTM_W,) * 4 + (MLSTM_H,) * 4 + (FOURIER_W,) + (CONV_W,) * 3 + (D_MODEL,) * N_BRANCH
N_IN = int(sum(IN_SPLITS))
IN_OFFSETS = tuple(int(o) for o in np.cumsum(IN_SPLITS)[:-1])
F_FWD_OFF = 4 * MLSTM_W + MLSTM_H
F_BWD_OFF = 4 * MLSTM_W + 3 * MLSTM_H

kernel_name = "hybrid_mlstm_fourier_conv_diffusion_step"


def layer_norm(x, g, b):
    xf = x.astype(jnp.float32)
    mu = xf.mean(-1, keepdims=True)
    var = jnp.square(xf - mu).mean(-1, keepdims=True)
    return ((xf - mu) * lax.rsqrt(var + LN_EPS) * g + b).astype(x.dtype)


def grid_pos_embed(rows, dtype):
    quarter = D_MODEL // 4
    freqs = POS_BASE ** (-jnp.arange(quarter, dtype=jnp.float32) / quarter)
    r = jnp.repeat(jnp.arange(rows, dtype=jnp.float32), GRID_W)[:, None] * freqs
    col = jnp.tile(jnp.arange(GRID_W, dtype=jnp.float32), rows)[:, None] * freqs
    return jnp.concatenate([jnp.sin(r), jnp.cos(r), jnp.sin(col), jnp.cos(col)], axis=-1).astype(dtype)


def mlstm_scan(q, k, v, i_pre, logf, C0, n0, m0):
    B, H, T, DH = q.shape
    nc = T // CHUNK
    def split(a):
        return jnp.moveaxis(a.reshape(B, H, nc, CHUNK, *a.shape[3:]), 2, 0)
    mask = jnp.tril(jnp.ones((CHUNK, CHUNK), dtype=bool))

    def step(carry, inp):
        C, n, m = carry
        qc, kc, vc, ic, fc = inp
        b = jnp.cumsum(fc, axis=-1)
        Dm = jnp.where(mask, b[..., :, None] - b[..., None, :] + ic[..., None, :], -jnp.inf)
        inter = b + m[..., None]
        m_t = jnp.maximum(inter, Dm.max(-1))
        s = jnp.einsum('bhtd,bhsd->bhts', qc, kc) * jnp.exp(Dm - m_t[..., None])
        a = jnp.exp(inter - m_t)
        num = a[..., None] * jnp.einsum('bhtd,bhde->bhte', qc, C) + jnp.einsum('bhts,bhse->bhte', s, vc)
        den = a * jnp.einsum('bhtd,bhd->bht', qc, n) + s.sum(-1)
        h = num / jnp.maximum(jnp.abs(den), jnp.exp(-m_t))[..., None]
        g = b[..., -1]
        dec = g[..., None] - b + ic
        m_new = jnp.maximum(g + m, dec.max(-1))
        a_c = jnp.exp(g + m - m_new)
        wk = jnp.exp(dec - m_new[..., None])
        C_new = a_c[..., None, None] * C + jnp.einsum('bhs,bhsd,bhse->bhde', wk, kc, vc)
        n_new = a_c[..., None] * n + jnp.einsum('bhs,bhsd->bhd', wk, kc)
        return (C_new, n_new, m_new), h

    carry0 = (C0.astype(jnp.float32), n0.astype(jnp.float32), m0.astype(jnp.float32))
    (C, n, m), hs = lax.scan(step, carry0, (split(q), split(k), split(v), split(i_pre), split(logf)))
    h = jnp.moveaxis(hs, 0, 2).reshape(B, H, T, DH)
    return h, (C, n, m)


def mlstm_bidir(q, k, v, i_f, lf_f, i_b, lf_b, C0, n0, m0):
    h_f, s_f = mlstm_scan(q, k, v, i_f, lf_f, C0[:, 0], n0[:, 0], m0[:, 0])
    rev = lambda a: jnp.flip(a, axis=2)
    h_b, s_b = mlstm_scan(rev(q), rev(k), rev(v), rev(i_b), rev(lf_b), C0[:, 1], n0[:, 1], m0[:, 1])
    states = (jnp.stack([s_f[0], s_b[0]], axis=1), jnp.stack([s_f[1], s_b[1]], axis=1),
              jnp.stack([s_f[2], s_b[2]], axis=1))
    return h_f + rev(h_b), states


def conv3(u, w, b):
    up = jnp.pad(u, ((0, 0), (1, 1), (0, 0)))
    return up[:, :-2] * w[0] + up[:, 1:-1] * w[1] + up[:, 2:] * w[2] + b


def token_mixer(u, l, P, init_state, rows):
    B, T, _ = u.shape
    f32 = jnp.float32
    z = u @ P['w_in'][l] + P['b_in'][l]
    (q, k, v, o, i_f, f_f, i_b, f_b, xfo, cb, cc, cx, gm, gf, gc) = jnp.split(z, IN_OFFSETS, axis=-1)
    heads = lambda a: a.reshape(B, T, MLSTM_H, MLSTM_DH).transpose(0, 2, 1, 3).astype(f32)
    gate = lambda a: a.transpose(0, 2, 1).astype(f32)
    h, new_state = mlstm_bidir(heads(q), heads(k) * (MLSTM_DH ** -0.5), heads(v),
                               gate(i_f), jax.nn.log_sigmoid(gate(f_f)),
                               gate(i_b), jax.nn.log_sigmoid(gate(f_b)), *init_state)
    mu = h.mean(-1, keepdims=True)
    var = jnp.square(h - mu).mean(-1, keepdims=True)
    h = ((h - mu) * lax.rsqrt(var + LN_EPS)).transpose(0, 2, 1, 3).reshape(B, T, MLSTM_W)
    h_m = (jax.nn.sigmoid(o.astype(f32)) * h * P['mlstm_norm_g'][l]).astype(u.dtype)
    xg = xfo.reshape(B, T, FOURIER_G, FOURIER_GW).astype(f32)
    h_f = jnp.fft.fftn(xg, axes=(1, 3), norm='ortho').real.reshape(B, T, FOURIER_W).astype(u.dtype)
    uc = cc * cx
    if rows is None:
        yc = conv3(uc, P['conv_w'][l], P['conv_b'][l])
    else:
        yc = conv3(uc.reshape(B * rows, GRID_W, CONV_W), P['conv_w'][l], P['conv_b'][l]).reshape(B, T, CONV_W)
    h_c = cb * yc
    merged = (jax.nn.sigmoid(gm) * (h_m @ P['w_br_mlstm'][l])
              + jax.nn.sigmoid(gf) * (h_f @ P['w_br_fourier'][l])
              + jax.nn.sigmoid(gc) * (h_c @ P['w_br_conv'][l]))
    return merged @ P['w_out'][l], new_state


def swiglu(u, w_gate_up, w_down):
    a, b = jnp.split(u @ w_gate_up, 2, axis=-1)
    return (jax.nn.silu(a) * b) @ w_down


def ada_mod(cond, l, P):
    m = jax.nn.silu(cond) @ P['w_ada'][l] + P['b_ada'][l]
    return [a[:, None, :] for a in jnp.split(m, 6, axis=-1)]


def trunk_layer(x, mods, l, P, init_state, rows):
    sh1, sc1, g1, sh2, sc2, g2 = mods
    y, st = token_mixer(x * (1 + sc1) + sh1, l, P, init_state, rows)
    x = layer_norm(ALPHA * x + g1 * y, P['ln_g'][l, 0], P['ln_b'][l, 0])
    y = swiglu(x * (1 + sc2) + sh2, P['w_gate_up'][l], P['w_down'][l])
    x = layer_norm(ALPHA * x + g2 * y, P['ln_g'][l, 1], P['ln_b'][l, 1])
    return x, st


def setup_inputs(seed: int = 0) -> dict:
    key = jax.random.key(seed)
    ks = jax.random.split(key, 24)
    f32 = jnp.float32
    nrm = lambda k, shape, s: s * jax.random.normal(k, shape, f32)
    fgate_bias = jnp.linspace(3.0, 6.0, MLSTM_H, dtype=f32)
    b_in = nrm(ks[10], (DEPTH, N_IN), 0.02)
    b_in = b_in.at[:, F_FWD_OFF:F_FWD_OFF + MLSTM_H].add(fgate_bias)
    b_in = b_in.at[:, F_BWD_OFF:F_BWD_OFF + MLSTM_H].add(fgate_bias)
    return {
        'x_prompt': nrm(ks[0], (BATCH, SEQ, D_MODEL), 1.0),
        'x_sample': nrm(ks[1], (DEC_BATCH, DEC_SEQ, D_MODEL), 1.0),
        'state_C': nrm(ks[2], (DEC_BATCH, DEPTH, 2, MLSTM_H, MLSTM_DH, MLSTM_DH), 0.05),
        'state_n': nrm(ks[3], (DEC_BATCH, DEPTH, 2, MLSTM_H, MLSTM_DH), 0.05),
        'state_m': nrm(ks[4], (DEC_BATCH, DEPTH, 2, MLSTM_H), 0.5),
        'c': nrm(ks[5], (DEC_BATCH, D_MODEL), 1.0),
        'c_ctx': nrm(ks[6], (D_MODEL,), 1.0),
        'w_ada': nrm(ks[7], (DEPTH, D_MODEL, 6 * D_MODEL), 0.5 * D_MODEL ** -0.5),
        'b_ada': nrm(ks[8], (DEPTH, 6 * D_MODEL), 0.02),
        'w_in': nrm(ks[9], (DEPTH, D_MODEL, N_IN), D_MODEL ** -0.5),
        'b_in': b_in,
        'mlstm_norm_g': 1.0 + nrm(ks[11], (DEPTH, MLSTM_W), 0.02),
        'conv_w': nrm(ks[12], (DEPTH, 3, CONV_W), 3 ** -0.5),
        'conv_b': nrm(ks[13], (DEPTH, CONV_W), 0.02),
        'w_br_mlstm': nrm(ks[14], (DEPTH, MLSTM_W, D_MODEL), BETA * MLSTM_W ** -0.5),
        'w_br_fourier': nrm(ks[15], (DEPTH, FOURIER_W, D_MODEL), BETA * FOURIER_W ** -0.5),
        'w_br_conv': nrm(ks[16], (DEPTH, CONV_W, D_MODEL), BETA * CONV_W ** -0.5),
        'w_out': nrm(ks[17], (DEPTH, D_MODEL, D_MODEL), BETA * D_MODEL ** -0.5),
        'ln_g': 1.0 + nrm(ks[18], (DEPTH, 2, D_MODEL), 0.02),
        'ln_b': nrm(ks[19], (DEPTH, 2, D_MODEL), 0.02),
        'w_gate_up': nrm(ks[20], (DEPTH, D_MODEL, 2 * D_FF), D_MODEL ** -0.5),
        'w_down': nrm(ks[21], (DEPTH, D_FF, D_MODEL), BETA * D_FF ** -0.5),
    }


def reference(x_prompt, x_sample, state_C, state_n, state_m, c, c_ctx, w_ada, b_ada, w_in, b_in,
              mlstm_norm_g, conv_w, conv_b, w_br_mlstm, w_br_fourier, w_br_conv, w_out, ln_g, ln_b,
              w_gate_up, w_down):
    P = {'w_ada': w_ada, 'b_ada': b_ada, 'w_in': w_in, 'b_in': b_in, 'mlstm_norm_g': mlstm_norm_g,
         'conv_w': conv_w, 'conv_b': conv_b, 'w_br_mlstm': w_br_mlstm, 'w_br_fourier': w_br_fourier,
         'w_br_conv': w_br_conv, 'w_out': w_out, 'ln_g': ln_g, 'ln_b': ln_b,
         'w_gate_up': w_gate_up, 'w_down': w_down}
    f32 = jnp.float32
    Bp = x_prompt.shape[0]
    zero_state = (jnp.zeros((Bp, 2, MLSTM_H, MLSTM_DH, MLSTM_DH), f32),
                  jnp.zeros((Bp, 2, MLSTM_H, MLSTM_DH), f32),
                  jnp.zeros((Bp, 2, MLSTM_H), f32))
    xp = x_prompt
    Cs, ns, ms = [], [], []
    for l in range(DEPTH):
        xp, (Cl, nl, ml) = trunk_layer(xp, ada_mod(c_ctx[None, :], l, P), l, P, zero_state, None)
        Cs.append(Cl)
        ns.append(nl)
        ms.append(ml)
    y_prompt = xp
    new_state_C = jnp.stack(Cs, axis=1)
    new_state_n = jnp.stack(ns, axis=1)
    new_state_m = jnp.stack(ms, axis=1)
    rows = x_sample.shape[1] // GRID_W
    xs = x_sample + grid_pos_embed(rows, x_sample.dtype)
    for l in range(DEPTH):
        xs, _ = trunk_layer(xs, ada_mod(c, l, P), l, P,
                            (state_C[:, l], state_n[:, l], state_m[:, l]), rows)
    y_sample = xs
    return (y_prompt, y_sample, new_state_C, new_state_n, new_state_m)
```

```python
import numpy as np
from contextlib import ExitStack
import concourse.bass as bass
import concourse.mybir as mybir
from concourse.bass_utils import run_bass_kernel_spmd

F32 = mybir.dt.float32
BF16 = mybir.dt.bfloat16
F32R = mybir.dt.float32r
AF = mybir.ActivationFunctionType
ALU = mybir.AluOpType
AX = mybir.AxisListType

D = 1024
NTOK = 1536
NCH = 12
DEPTH = 2
H = 4
DH = 256
D_FF = 2816
NFC = 22
N_IN = 11280
Q_OFF, K_OFF, V_OFF, O_OFF, G_OFF, F_OFF = 0, 1024, 2048, 3072, 4096, 4112
CB_OFF, CC_OFF, CX_OFF, GM_OFF, GF_OFF, GC_OFF = 5136, 6160, 7184, 8208, 9232, 10256
ALPHA = (2 * DEPTH) ** 0.25
LN_EPS = 1e-5
BLK = {"q": 0, "k": 1, "o": 2, "f": 3, "cb": 4, "cc": 5, "cx": 6, "gm": 7, "gf": 8, "gc": 9}
BLK_OFF = {"q": Q_OFF, "k": K_OFF, "o": O_OFF, "f": F_OFF, "cb": CB_OFF, "cc": CC_OFF, "cx": CX_OFF,
           "gm": GM_OFF, "gf": GF_OFF, "gc": GC_OFF}
C_MNG, C_W0, C_W1, C_W2, C_CB, C_LNG0, C_LNB0, C_LNG1, C_LNB1, NCOL = 80, 88, 96, 104, 112, 120, 128, 136, 144, 152


def tile_specs():
    T = []
    for l in range(DEPTH):
        if l == 0:
            for t in range(12):
                T.append((8, 512, [("w_ada", l, t * 512, 512, 0)]))
        T.append((8, 16, [("w_in", l, G_OFF, 16, 0)]))
        for h in range(H):
            T.append((8, 512, [("w_in", l, Q_OFF + h * 256, 256, 0), ("w_in", l, K_OFF + h * 256, 256, 256)]))
            T.append((8, 512, [("w_in", l, V_OFF + h * 256, 256, 0), ("w_in", l, O_OFF + h * 256, 256, 256)]))

        def branch(wname, goff):
            for jq in range(4):
                T.append((8, 512, [(wname, l, jq * 256, 256, 0), ("w_in", l, goff + jq * 256, 256, 256)]))
        branch("w_br_mlstm", GM_OFF)
        for g in range(4):
            T.append((8, 256, [("w_in", l, F_OFF + g * 256, 256, 0)]))
            for tq in range(4):
                T.append((8, 512, [("tA", 0, tq * 256, 256, 0), ("tA", 1, tq * 256, 256, 256)]))
        branch("w_br_fourier", GF_OFF)
        for j in range(8):
            T.append((8, 384, [("w_in", l, CC_OFF + j * 128, 128, 0), ("w_in", l, CX_OFF + j * 128, 128, 128),
                               ("w_in", l, CB_OFF + j * 128, 128, 256)]))
        branch("w_br_conv", GC_OFF)
        for jj in range(2):
            T.append((8, 512, [("w_out", l, jj * 512, 512, 0)]))
        for f2 in range(11):
            T.append((8, 512, [("w_gate_up", l, f2 * 256, 256, 0), ("w_gate_up", l, D_FF + f2 * 256, 256, 256)]))
            if l + 1 < DEPTH:
                T.append((8, 512, [("w_ada", l + 1, f2 * 512, 512, 0)]))
        for j in range(8):
            T.append((NFC, 128, [("w_down", l, j * 128, 128, 0)]))
            if j == 0 and l + 1 < DEPTH:
                T.append((8, 512, [("w_ada", l + 1, 11 * 512, 512, 0)]))
    offs = []
    o = 0
    for (K, w, _) in T:
        offs.append(o)
        o += K * w
    return T, offs, o


def pack_tA(tA):
    import ml_dtypes
    out = np.zeros((128, 4, 8, 512), np.float32)
    for tq in range(4):
        for m in range(2):
            out[:, tq, :, m * 256:(m + 1) * 256] = tA[m][:, tq * 256:(tq + 1) * 256].reshape(8, 128, 256).transpose(1, 0, 2)
    return np.ascontiguousarray(out.reshape(128, 4 * 4096).astype(ml_dtypes.bfloat16))


def pack_weights(W, tA=None):
    T, offs, total = tile_specs()
    out = np.zeros((128, total), np.float32)
    for (K, w, pieces), o in zip(T, offs):
        blk = out[:, o:o + K * w].reshape(128, K, w)
        for (name, l, c0, n, dst) in pieces:
            if name == "tA":
                continue
            src = W[name][l]
            blk[:, :, dst:dst + n] = src[:, c0:c0 + n].reshape(K, 128, n).transpose(1, 0, 2)
    return out

class TL:
    def __init__(self, sem, step, name):
        self.sem, self.step, self.cnt, self.name = sem, step, 0, name


class Buf:
    __slots__ = ("name", "w", "rd")

    def __init__(self, name=""):
        self.name, self.w, self.rd = name, None, []


def _compact(rd):
    best = {}
    for tl, v in rd:
        if best.get(tl, 0) < v:
            best[tl] = v
    return list(best.items())


class Sched:
    def __init__(self, nc, stack):
        self.nc = nc
        self.stack = stack
        self.eng = {"pe": nc.tensor, "dve": nc.vector, "act": nc.scalar, "pool": nc.gpsimd, "sp": nc.sync}
        self.tl = {}
        for k in self.eng:
            self.tl[k] = TL(stack.enter_context(nc.semaphore("s_" + k)), 1, k)
        self.seen = {k: {} for k in self.eng}
        self.dma_tls = []

    def dma_tl(self, name):
        t = TL(self.stack.enter_context(self.nc.semaphore("d_" + name)), 16, name)
        self.dma_tls.append(t)
        return t

    def _wait(self, e, deps):
        seen = self.seen[e]
        for tl, v in deps:
            if v <= 0 or seen.get(tl, 0) >= v:
                continue
            if tl.step == 16:
                v = tl.cnt
            self.eng[e].wait_ge(tl.sem, v)
            seen[tl] = v

    def _deps(self, e, reads, writes, is_dma=False):
        deps = {}
        mytl = self.tl[e]

        def add(tl, v):
            if deps.get(tl, 0) < v:
                deps[tl] = v
        for b in reads:
            if b.w is not None:
                tl, v = b.w
                if tl is mytl and e == "pe" and not is_dma:
                    continue
                add(tl, v)
        for b in writes:
            if b.w is not None:
                tl, v = b.w
                if tl is not mytl or is_dma or e != "pe":
                    add(tl, v)
            for tl, v in b.rd:
                if tl is not mytl or is_dma or e != "pe":
                    add(tl, v)
        return list(deps.items())

    def op(self, e, fn, reads=(), writes=(), mark=True):
        self._wait(e, self._deps(e, reads, writes))
        ins = fn()
        tl = self.tl[e]
        if mark:
            tl.cnt += 1
            ins.then_inc(tl.sem, 1)
            v = tl.cnt
        else:
            v = tl.cnt + 1
        for b in reads:
            b.rd.append((tl, v))
            if len(b.rd) > 32:
                b.rd = _compact(b.rd)
        for b in writes:
            b.w = (tl, v)
            b.rd = []
        return ins

    def dma(self, e, out, in_, dtl, reads=(), writes=(), **kw):
        self._wait(e, self._deps(e, reads, writes, is_dma=True))
        ins = self.eng[e].dma_start(out=out, in_=in_, **kw)
        dtl.cnt += 16
        ins.then_inc(dtl.sem, 16)
        v = dtl.cnt
        for b in reads:
            b.rd.append((dtl, v))
        for b in writes:
            b.w = (dtl, v)
            b.rd = []
        return ins

    def barrier(self, engines=("dve", "act", "sp")):
        for e in engines:
            deps = [(self.tl[k], self.tl[k].cnt) for k in self.eng if k != e]
            deps += [(t, t.cnt) for t in self.dma_tls]
            self._wait(e, deps)

    def final_wait(self, e="sp"):
        deps = [(self.tl[k], self.tl[k].cnt) for k in self.eng if k != e]
        deps += [(t, t.cnt) for t in self.dma_tls]
        self._wait(e, deps)


def build_program():
    nc = bass.Bass("TRN2", target_bir_lowering=False)

    def din(name, shape):
        return nc.dram_tensor(name, list(shape), F32, kind="ExternalInput").ap()

    def dout(name, shape):
        return nc.dram_tensor(name, list(shape), F32, kind="ExternalOutput").ap()

    xin = din("xin", [NTOK, D])
    pos = din("pos", [1024, D])
    condT = din("condT", [D, 2])
    C0 = din("C0", [DEPTH, 2, H, DH, DH])
    n0T = din("n0T", [DEPTH, 128, 2, 8])
    m0 = din("m0", [DEPTH, 1, 8])
    keep_d = din("keep", [1, 96])
    TSPEC, TOFF, TTOTAL = tile_specs()
    wpack = din("wpack", [128, TTOTAL])
    tApack = nc.dram_tensor("tApack", [128, 4 * 4096], BF16, kind="ExternalInput").ap()
    badaT = din("badaT", [DEPTH, 128, 96])
    colp_d = din("colp", [DEPTH, 128, NCOL])
    bg_bc = din("bg_bc", [DEPTH, 128, 192])
    bkv_bc = din("bkv_bc", [DEPTH, H, 128, 1024])
    cmat_d = din("cmat", [128, 512])
    dcs_d = din("dcs", [256, 512])
    tB_d = din("tB", [2, 256, 256])
    cmask_d = din("cmask", [128, 2 * NTOK])
    y_d = dout("y", [NTOK, D])
    oC_d = dout("oC", [6, DEPTH, 2, H, DH, DH])
    on_d = dout("on_raw", [DEPTH, 128, 96])
    om_d = dout("om_raw", [DEPTH, 1, 96])

    with ExitStack() as st:
        S = Sched(nc, st)
        V, A, PE = nc.vector, nc.scalar, nc.tensor

        uid = {"n": 0}

        def sb(name, shape, dt=F32, stack=None):
            uid["n"] += 1
            return (stack or st).enter_context(nc.sbuf_tensor("sb%d_%s" % (uid["n"], name), list(shape), dt))

        xT = sb("xT", [128, 8, NTOK]); b_x = [Buf("x%d" % i) for i in range(3)]
        uT = sb("uT", [128, 8, NTOK], BF16); b_u = [Buf("u%d" % i) for i in range(3)]
        NS = 3
        ring = [sb("ring%d" % i, [128, 4096], BF16) for i in range(NS)]
        ring_b = [Buf("ring%d" % i) for i in range(NS)]
        ring_tl = [S.dma_tl("ring%d" % i) for i in range(NS)]
        cmat = sb("cmat", [128, 512]); b_cmat = Buf("cmat")
        cmb = sb("cmb", [128, 384], BF16); b_cmb = Buf("cmb")
        onesb = sb("onesb", [128, 2], BF16)
        cst = sb("cst", [128, 4])
        onesr = sb("onesr", [128, 128], F32R)
        colp = sb("colp", [128, DEPTH, NCOL]); b_colp = Buf("colp")
        kb16 = sb("kb16", [128, DEPTH, 8]); b_kb16 = Buf("kb16")
        modT_all = sb("modT", [128, DEPTH, 96]); b_mods = [Buf("mod%d" % i) for i in range(DEPTH)]
        g1a_all = sb("g1a", [128, DEPTH, 32]); b_g1as = [Buf("g1a%d" % i) for i in range(DEPTH)]
        bada_all = sb("bada", [128, DEPTH, 96]); b_bada = Buf("bada")
        tB = sb("tB", [128, 2, 2, 256], BF16); b_tB = Buf("tB")
        dcs = sb("dcs", [128, 2, 512], BF16); b_dcs = Buf("dcs")
        rows = sb("rows", [1, 1024]); b_rows = Buf("rows")
        ones_row = sb("ones_row", [1, 128])
        tabs = sb("tabs", [128, 8, 96]); b_tabs = [Buf("tab%d" % i) for i in range(8)]
        gates = sb("gates", [128, 192]); b_gates = Buf("gates")
        nall = sb("nall", [128, 2, 48]); b_nall = Buf("nall")
        scT = sb("scT", [128, 8, 2], BF16); b_scT = Buf("scT")
        cmat_tl, colp_tl, bada_tl, ct_tl, bgt_tl, rows_tl = [S.dma_tl(n) for n in ("cmat", "colp", "bada", "ct", "bgt", "rows")]
        tB_tl, dcs_tl, cmask_tl = [S.dma_tl(n) for n in ("tB", "dcs", "cmask")]
        out_tl = S.dma_tl("out")
        oc_tls = [S.dma_tl("oc0"), S.dma_tl("oc1")]
        st_tls = [S.dma_tl("cst0"), S.dma_tl("cst1")]
        ys_tls = [S.dma_tl("ys0"), S.dma_tl("ys1")]
        bkv_tl = S.dma_tl("bkv")

        pbank = [st.enter_context(nc.psum_tensor("pb%d" % i, [128, 512], F32)) for i in range(7)]
        pb_b = [Buf("pb%d" % i) for i in range(7)]
        ptb = st.enter_context(nc.psum_tensor("ptb", [128, 1024], BF16))
        ptb_b = [Buf("ptb%d" % i) for i in range(4)]
        pstate = {"i": 0, "t": 0}

        def PB():
            i = pstate["i"]
            pstate["i"] = (i + 1) % 6
            return pbank[i], pb_b[i]

        def PT():
            i = pstate["t"]
            pstate["t"] = (i + 1) % 4
            return ptb[:, i * 256:(i + 1) * 256], ptb_b[i]

        TRIF, TRIB, IDN, ONESM = (cmat[:, 0:128], cmat[:, 128:256], cmat[:, 256:384], cmat[:, 384:512])
        TRIFb, TRIBb, IDNb = (cmb[:, 0:128], cmb[:, 128:256], cmb[:, 256:384])

        wq = {"n": 0, "plan": [], "issued": 0}

        def _issue(idx):
            K_, w_, _ = TSPEC[idx]
            n = K_ * w_
            ch = n if n <= 2048 else n // 2
            assert ch <= 2048 and n % ch == 0
            sl_ = idx % NS
            pieces_ = TSPEC[idx][2]
            if pieces_[0][0] == "tA":
                tq_ = pieces_[0][2] // 256
                src_ = tApack[:, tq_ * 4096:(tq_ + 1) * 4096]
            else:
                src_ = wpack[:, TOFF[idx]:TOFF[idx] + n]
            S.dma("pool", ring[sl_][:, 0:n].rearrange("p (a c) -> p a c", c=ch),
                  src_.rearrange("p (a c) -> p a c", c=ch), ring_tl[sl_], writes=[ring_b[sl_]])

        def wnext(keep=0):
            i = wq["n"]
            released = i - keep
            assert i < released + NS
            while wq["issued"] < min(len(wq["plan"]), released + NS):
                _issue(wq["issued"])
                wq["issued"] += 1
            wq["n"] = i + 1
            return ring[i % NS], ring_b[i % NS]

        wq["plan"] = list(range(len(TSPEC)))
        wq["n"] = 12
        wq["issued"] = 12

        def rv(r, K=8, width=512):
            return r[:, 0:K * width].rearrange("p (k n) -> p k n", k=K)

        S.dma("sp", cmat[:], cmat_d, cmat_tl, writes=[b_cmat])
        S.dma("sp", colp[:], colp_d.rearrange("l p c -> p l c"), colp_tl, writes=[b_colp])
        S.dma("sp", bada_all[:], badaT.rearrange("l p c -> p l c"), bada_tl, writes=[b_bada])
        for m_ in range(2):
            S.dma("pool", tB[:, :, m_, :], tB_d[m_].rearrange("(k p) n -> p k n", p=128), tB_tl, writes=[b_tB])
        S.dma("pool", dcs[:], dcs_d.rearrange("(k p) n -> p k n", p=128), dcs_tl, writes=[b_dcs])
        S.op("dve", lambda: V.tensor_copy(out=cmb[:], in_=cmat[:, 0:384]), reads=[b_cmat], writes=[b_cmb])
        b_ones = Buf("ones")
        S.op("dve", lambda: V.memset(onesb[:], 1.0), writes=[b_ones])
        S.op("dve", lambda: V.memset(ones_row[:], 1.0), writes=[b_ones])
        S.op("dve", lambda: V.memset(cst[:, 0:1], 1.0), writes=[b_ones])
        S.op("act", lambda: A.activation(out=onesr[:], in_=cmat[:, 384:512], func=AF.Copy), reads=[b_cmat], writes=[b_ones])
        S.op("dve", lambda: V.memset(cst[:, 1:2], LN_EPS), writes=[b_ones])
        S.op("dve", lambda: V.memset(cst[:, 2:3], LN_EPS / (ALPHA * ALPHA)), writes=[b_ones])
        S.op("dve", lambda: V.tensor_scalar(out=kb16[:], in0=colp[:, :, 8:16], scalar1=1.0 / 16, scalar2=None,
                                            op0=ALU.mult), reads=[b_colp], writes=[b_kb16])
        with ExitStack() as ph:
            ct = sb("ct", [128, 8, 2], stack=ph); b_ct = Buf("ct")
            S.dma("sp", ct[:], condT.rearrange("(k p) j -> p k j", p=128), ct_tl, writes=[b_ct])
            S.op("act", lambda: A.activation(out=scT[:], in_=ct[:], func=AF.Silu), reads=[b_ct], writes=[b_scT])
            xs = [sb("xs%d" % i, [128, D], stack=ph) for i in range(2)]; b_xs = [Buf("xs0"), Buf("xs1")]
            ps_ = [sb("ps%d" % i, [128, D], stack=ph) for i in range(2)]; b_ps = [Buf("ps0"), Buf("ps1")]
            xs_tl = [S.dma_tl("xs0"), S.dma_tl("xs1")]
            for c in range(NCH):
                i = c % 2
                S.dma("sp", xs[i][:], xin[c * 128:(c + 1) * 128, :], xs_tl[i], writes=[b_xs[i]])
                if c < 8:
                    S.dma("sp", ps_[i][:], pos[c * 128:(c + 1) * 128, :], xs_tl[i], writes=[b_ps[i]])
                    S.op("dve", lambda: V.tensor_tensor(out=xs[i][:], in0=xs[i][:], in1=ps_[i][:], op=ALU.add),
                         reads=[b_xs[i], b_ps[i]], writes=[b_xs[i]])
                for k2 in range(2):
                    pb, bpb = PB()
                    for kk in range(4):
                        k = k2 * 4 + kk
                        S.op("pe", lambda: PE.transpose(out=pb[:, kk * 128:(kk + 1) * 128],
                                                        in_=xs[i][:, k * 128:(k + 1) * 128], identity=IDN),
                             reads=[b_xs[i], b_cmat], writes=[bpb], mark=(kk == 3))
                    e = "act" if k2 == 0 else "dve"
                    dst = xT[:, k2 * 4:k2 * 4 + 4, c * 128:(c + 1) * 128]
                    src = pb[:, :].rearrange("p (k n) -> p k n", k=4)
                    if e == "act":
                        S.op("act", lambda: A.activation(out=dst, in_=src, func=AF.Copy), reads=[bpb], writes=[b_x[c // 4]])
                    else:
                        S.op("dve", lambda: V.tensor_copy(out=dst, in_=src), reads=[bpb], writes=[b_x[c // 4]])
            S.barrier()

        blk_of_unit = {0: [0, 1], 1: [2]}

        def bcol(l, name, j):
            c = BLK[name] * 8 + j
            return colp[:, l, c:c + 1]

        def proj_fm(l, lhs_fn, rhs_t, rhs_b, blk, K=8):
            pb, bpb = PB()
            for k in range(K):
                S.op("pe", lambda: PE.matmul(pb[:, :], lhsT=lhs_fn(k), rhs=rhs_t[:, k, blk * 512:(blk + 1) * 512],
                                             start=(k == 0), stop=(k == K - 1)),
                     reads=rhs_b, writes=[bpb], mark=(k == K - 1))
            return pb, bpb

        for l in range(DEPTH):
            pbm, bpbm = pbank[6], pb_b[6]

            def ada_tile(la, t):
                r, rb = wnext()
                r3 = rv(r)
                for q in range(4):
                    n = t * 4 + q
                    for k in range(8):
                        S.op("pe", lambda: PE.matmul(pbm[:, n * 2:n * 2 + 2], lhsT=r3[:, k, q * 128:(q + 1) * 128],
                                                     rhs=scT[:, k, :], start=(k == 0), stop=(k == 7)),
                             reads=[rb, b_scT], writes=[bpbm], mark=(k == 7))

            def ada_finish(la):
                mT = modT_all[:, la, :]
                S.op("dve", lambda: V.tensor_tensor(out=mT, in0=pbm[:, 0:96], in1=bada_all[:, la, :], op=ALU.add),
                     reads=[bpbm, b_bada], writes=[b_mods[la]])
                for kind in (1, 4):
                    S.op("dve", lambda: V.tensor_scalar(out=mT[:, kind * 16:kind * 16 + 16], in0=mT[:, kind * 16:kind * 16 + 16],
                                                        scalar1=1.0, scalar2=None, op0=ALU.add), reads=[b_mods[la]], writes=[b_mods[la]])
                for gi, kind in enumerate((2, 5)):
                    S.op("dve", lambda: V.tensor_scalar(out=g1a_all[:, la, gi * 16:gi * 16 + 16], in0=mT[:, kind * 16:kind * 16 + 16],
                                                        scalar1=1.0 / ALPHA, scalar2=None, op0=ALU.mult), reads=[b_mods[la]], writes=[b_g1as[la]])
            if l == 0:
                with ExitStack() as ph:
                    NA = 3
                    stg = [sb("astg%d" % i, [128, 4096], stack=ph) for i in range(NA)]; b_stg = [Buf("astg%d" % i) for i in range(NA)]
                    stg_tl = [S.dma_tl("astg%d" % i) for i in range(NA)]
                    wbf = [sb("awb%d" % i, [128, 4096], BF16, stack=ph) for i in range(2)]; b_wbf = [Buf("awb0"), Buf("awb1")]

                    def a_issue(t):
                        S.dma("sp", stg[t % NA][:], wpack[:, TOFF[t]:TOFF[t] + 4096], stg_tl[t % NA], writes=[b_stg[t % NA]])
                    for t in range(NA):
                        a_issue(t)
                    for t in range(12):
                        si, wi_ = t % NA, t % 2
                        S.op("dve", lambda: V.tensor_copy(out=wbf[wi_][:, 0:2048], in_=stg[si][:, 0:2048]), reads=[b_stg[si]], writes=[b_wbf[wi_]])
                        S.op("act", lambda: A.activation(out=wbf[wi_][:, 2048:4096], in_=stg[si][:, 2048:4096], func=AF.Copy), reads=[b_stg[si]], writes=[b_wbf[wi_]])
                        if t + NA < 12:
                            a_issue(t + NA)
                        r3 = rv(wbf[wi_])
                        for q in range(4):
                            n = t * 4 + q
                            for k in range(8):
                                S.op("pe", lambda: PE.matmul(pbm[:, n * 2:n * 2 + 2], lhsT=r3[:, k, q * 128:(q + 1) * 128],
                                                             rhs=scT[:, k, :], start=(k == 0), stop=(k == 7)),
                                     reads=[b_wbf[wi_], b_scT], writes=[bpbm], mark=(k == 7))
                    ada_finish(0)
                    S.barrier()
            modT = modT_all[:, l, :]
            b_mod = b_mods[l]
            b_g1a = b_g1as[l]
            g1a = g1a_all[:, l, :].rearrange("p (g k j) -> p g k j", g=2, k=8)

            def mcol(kind, k, unit):
                c = (kind * 8 + k) * 2 + unit
                return modT[:, c:c + 1]

            def modulate(sh_kind, sc_kind):
                for k in range(8):
                    for unit, (t0, t1) in enumerate(((0, 1024), (1024, 1536))):
                        bl = [b_x[0], b_x[1]] if unit == 0 else [b_x[2]]
                        bu = [b_u[0], b_u[1]] if unit == 0 else [b_u[2]]
                        S.op("dve", lambda: V.tensor_scalar(out=uT[:, k, t0:t1], in0=xT[:, k, t0:t1],
                                                            scalar1=mcol(sc_kind, k, unit), scalar2=mcol(sh_kind, k, unit),
                                                            op0=ALU.mult, op1=ALU.add), reads=bl + [b_mod], writes=bu)
            modulate(0, 1)

            def residual_ln(which, proj_w_fn, rhs_t, rhs_bf, K):
                for j in range(8):
                    lhs_fn = proj_w_fn(j)
                    for blk in range(3):
                        unit = 0 if blk < 2 else 1
                        pb, bpb = proj_fm(l, lhs_fn[0], rhs_t, [rhs_bf[blk], lhs_fn[1]], blk, K=K)
                        S.op("dve", lambda: V.scalar_tensor_tensor(out=xT[:, j, blk * 512:(blk + 1) * 512], in0=pb[:, :],
                                                                   scalar=g1a[:, which, j, unit:unit + 1],
                                                                   in1=xT[:, j, blk * 512:(blk + 1) * 512],
                                                                   op0=ALU.mult, op1=ALU.add),
                             reads=[bpb, b_g1a, b_x[blk]], writes=[b_x[blk]])
                with ExitStack() as ph:
                    sq = [sb("sq%d" % i, [128, 512], stack=ph) for i in range(2)]; b_sq = [Buf("sq0"), Buf("sq1")]
                    rstd = [sb("rstd%d" % i, [128, 512], stack=ph) for i in range(3)]; b_rstd = [Buf("rstd%d" % i) for i in range(3)]
                    cg = C_LNG0 if which == 0 else C_LNG1
                    cb_ = C_LNB0 if which == 0 else C_LNB1
                    sls = [slice(blk * 512, (blk + 1) * 512) for blk in range(3)]
                    pms = []
                    for blk in range(3):
                        pm, bpm = PB()
                        pms.append((pm, bpm))
                        for k in range(8):
                            S.op("pe", lambda: PE.matmul(pm[:, :], lhsT=ONESM, rhs=xT[:, k, sls[blk]], start=(k == 0), stop=(k == 7)),
                                 reads=[b_x[blk], b_cmat], writes=[bpm], mark=(k == 7))
                    for blk in range(3):
                        pm, bpm = pms[blk]
                        for k in range(8):
                            S.op("dve", lambda: V.tensor_tensor(out=xT[:, k, sls[blk]], in0=xT[:, k, sls[blk]], in1=pm[:, :], op=ALU.subtract),
                                 reads=[bpm, b_x[blk]], writes=[b_x[blk]])
                    pvs = []
                    n_ = 0
                    for blk in range(3):
                        pv, bpv = PB()
                        pvs.append((pv, bpv))
                        for k in range(8):
                            i = n_ % 2
                            n_ += 1
                            S.op("act", lambda: A.activation(out=sq[i][:].bitcast(F32R), in_=xT[:, k, sls[blk]], func=AF.Square),
                                 reads=[b_x[blk]], writes=[b_sq[i]])
                            S.op("pe", lambda: PE.matmul(pv[:, :], lhsT=onesr[:], rhs=sq[i][:].bitcast(F32R), start=(k == 0), stop=(k == 7)),
                                 reads=[b_sq[i], b_ones], writes=[bpv], mark=True)
                    for blk in range(3):
                        pv, bpv = pvs[blk]
                        S.op("act", lambda: A.activation(out=rstd[blk][:], in_=pv[:, :], func=AF.Sqrt, bias=cst[:, 2:3]), reads=[bpv, b_ones], writes=[b_rstd[blk]])
                        S.op("dve", lambda: V.reciprocal(out=rstd[blk][:], in_=rstd[blk][:]), reads=[b_rstd[blk]], writes=[b_rstd[blk]])
                    for blk in range(3):
                        for k in range(8):
                            S.op("dve", lambda: V.tensor_tensor(out=xT[:, k, sls[blk]], in0=xT[:, k, sls[blk]], in1=rstd[blk][:], op=ALU.mult),
                                 reads=[b_rstd[blk], b_x[blk]], writes=[b_x[blk]])
                            S.op("act", lambda: A.activation(out=xT[:, k, sls[blk]], in_=xT[:, k, sls[blk]], func=AF.Identity,
                                                             scale=colp[:, l, cg + k:cg + k + 1], bias=colp[:, l, cb_ + k:cb_ + k + 1]),
                                 reads=[b_x[blk], b_colp], writes=[b_x[blk]])
                    S.barrier()

            with ExitStack() as mix:
                brT = sb("brT", [128, 8, NTOK], BF16, stack=mix); b_br = [Buf("br%d" % i) for i in range(3)]

                r, rb = wnext()
                r3 = rv(r, 8, 16)
                pg, bpg = PB()
                for c in range(NCH):
                    for k in range(8):
                        S.op("pe", lambda: PE.matmul(pg[:, c * 16:(c + 1) * 16], lhsT=uT[:, k, c * 128:(c + 1) * 128], rhs=r3[:, k, 0:16],
                                                     start=(k == 0), stop=(k == 7)), reads=[rb, b_u[c // 4]], writes=[bpg], mark=(k == 7))
                T_L, T_B, T_D, T_W, T_W16, T_FL, T_A, T_X = range(8)
                with ExitStack() as ph:
                    bgt = sb("bgt", [128, 192], stack=ph); b_bgt = Buf("bgt")
                    S.dma("sp", bgt[:], bg_bc[l], bgt_tl, writes=[b_bgt])
                    S.op("dve", lambda: V.tensor_tensor(out=gates[:], in0=pg[:, 0:192], in1=bgt[:], op=ALU.add),
                         reads=[bpg, b_bgt], writes=[b_gates])
                    S.barrier()
                g3 = gates[:, :].rearrange("p (c n) -> p c n", n=16)

                def tab(i):
                    return tabs[:, i, :]

                def tab3(i):
                    return tabs[:, i, :].rearrange("p (c n) -> p c n", n=8)
                for d in range(2):
                    S.op("act", lambda: A.activation(out=tab3(T_L)[:, :, d * 4:d * 4 + 4], in_=g3[:, :, 4 + d * 8:8 + d * 8], func=AF.Exp, scale=-1.0),
                         reads=[b_gates], writes=[b_tabs[T_L]])
                S.op("act", lambda: A.activation(out=tab(T_L), in_=tab(T_L), func=AF.Ln, bias=cst[:, 0:1]), reads=[b_tabs[T_L], b_ones], writes=[b_tabs[T_L]])
                pc, bpc = PB()
                for d in range(2):
                    S.op("pe", lambda: PE.matmul(pc[:, d * 48:(d + 1) * 48].rearrange("p (c n) -> p c n", n=4), lhsT=(TRIF if d == 0 else TRIB),
                                                 rhs=tab3(T_L)[:, :, d * 4:d * 4 + 4], start=True, stop=True),
                         reads=[b_tabs[T_L], b_cmat], writes=[bpc], mark=True)
                for d in range(2):
                    S.op("dve", lambda: V.tensor_scalar(out=tab3(T_B)[:, :, d * 4:d * 4 + 4], in0=pc[:, d * 48:(d + 1) * 48].rearrange("p (c n) -> p c n", n=4),
                                                        scalar1=-1.0, scalar2=None, op0=ALU.mult), reads=[bpc], writes=[b_tabs[T_B]])
                for d in range(2):
                    S.op("dve", lambda: V.tensor_tensor(out=tab3(T_D)[:, :, d * 4:d * 4 + 4], in0=g3[:, :, d * 8:d * 8 + 4],
                                                        in1=tab3(T_B)[:, :, d * 4:d * 4 + 4], op=ALU.subtract),
                         reads=[b_gates, b_tabs[T_B]], writes=[b_tabs[T_D]])
                R_G, R_CM, R_KEEP, R_MP, R_MM, R_MN, R_A, R_T = [i * 96 for i in range(8)]
                pr, bpr = PB()
                S.op("pe", lambda: PE.matmul(pr[0:1, 0:96], lhsT=ONESM[:, 0:1], rhs=tab(T_L), start=True, stop=True),
                     reads=[b_tabs[T_L], b_cmat], writes=[bpr], mark=True)
                S.op("dve", lambda: V.tensor_scalar(out=rows[:, R_G:R_G + 96], in0=pr[0:1, 0:96], scalar1=-1024.0, scalar2=None, op0=ALU.mult),
                     reads=[bpr], writes=[b_rows])
                ptp, bptp = PB()
                S.op("pe", lambda: PE.transpose(out=ptp[0:96, 0:128], in_=tab(T_D), identity=IDN), reads=[b_tabs[T_D], b_cmat], writes=[bptp])
                with ExitStack() as ph:
                    cmc = sb("cmc", [96, 1], stack=ph); b_cmc = Buf("cmc")
                    S.op("dve", lambda: V.reduce_max(out=cmc[:], in_=ptp[0:96, 0:128], axis=AX.X), reads=[bptp], writes=[b_cmc])
                    pr2, bpr2 = PB()
                    S.op("pe", lambda: PE.matmul(pr2[0:1, 0:96], lhsT=cmc[:], rhs=IDN[0:96, 0:96], start=True, stop=True),
                         reads=[b_cmc, b_cmat], writes=[bpr2])
                    S.op("dve", lambda: V.tensor_copy(out=rows[:, R_CM:R_CM + 96], in_=pr2[0:1, 0:96]), reads=[bpr2], writes=[b_rows])
                    S.dma("sp", rows[:, R_KEEP:R_KEEP + 96], keep_d, rows_tl, writes=[b_rows])
                    S.dma("sp", rows[:, R_T:R_T + 8], m0[l], rows_tl, writes=[b_rows])
                    S.barrier()

                def rsl(base, c, d):
                    o = base + c * 8 + d * 4
                    return rows[:, o:o + 4]
                for d in range(2):
                    order = list(range(12)) if d == 0 else list(range(11, -1, -1))
                    start_c = 0 if d == 0 else 7
                    prev = None
                    for c in order:
                        carry = rows[:, R_T + d * 4:R_T + d * 4 + 4] if c == start_c else (rsl(R_MN, prev, d) if prev is not None else None)
                        if carry is None:
                            S.op("dve", lambda: V.memset(rsl(R_MP, c, d), 0.0), writes=[b_rows])
                        else:
                            S.op("dve", lambda: V.tensor_tensor(out=rsl(R_MP, c, d), in0=carry, in1=rsl(R_KEEP, c, d), op=ALU.mult),
                                 reads=[b_rows], writes=[b_rows])
                        S.op("dve", lambda: V.tensor_tensor(out=rsl(R_MM, c, d), in0=rsl(R_MP, c, d), in1=rsl(R_CM, c, d), op=ALU.max),
                             reads=[b_rows], writes=[b_rows])
                        S.op("dve", lambda: V.tensor_tensor(out=rsl(R_MN, c, d), in0=rsl(R_MM, c, d), in1=rsl(R_G, c, d), op=ALU.add),
                             reads=[b_rows], writes=[b_rows])
                        prev = c
                S.op("dve", lambda: V.tensor_tensor(out=rows[:, R_A:R_A + 96], in0=rows[:, R_MP:R_MP + 96], in1=rows[:, R_MM:R_MM + 96], op=ALU.subtract),
                     reads=[b_rows], writes=[b_rows])
                S.op("act", lambda: A.activation(out=rows[:, R_A:R_A + 96], in_=rows[:, R_A:R_A + 96], func=AF.Exp), reads=[b_rows], writes=[b_rows])
                S.op("dve", lambda: V.tensor_tensor(out=rows[:, R_A:R_A + 96], in0=rows[:, R_A:R_A + 96], in1=rows[:, R_KEEP:R_KEEP + 96], op=ALU.mult),
                     reads=[b_rows], writes=[b_rows])
                S.dma("sp", om_d[l], rows[:, R_MN:R_MN + 96], out_tl, reads=[b_rows])
                pbc, bpbc = PB()
                S.op("pe", lambda: PE.matmul(pbc[:, 0:96], lhsT=ones_row[:, :], rhs=rows[:, R_MM:R_MM + 96], start=True, stop=True),
                     reads=[b_rows, b_ones], writes=[bpbc], mark=False)
                S.op("pe", lambda: PE.matmul(pbc[:, 96:192], lhsT=ones_row[:, :], rhs=rows[:, R_A:R_A + 96], start=True, stop=True),
                     reads=[b_rows, b_ones], writes=[bpbc], mark=True)
                S.op("dve", lambda: V.tensor_copy(out=tab(T_A), in_=pbc[:, 96:192]), reads=[bpbc], writes=[b_tabs[T_A]])
                S.op("dve", lambda: V.tensor_tensor(out=tab(T_W), in0=tab(T_D), in1=pbc[:, 0:96], op=ALU.subtract),
                     reads=[bpbc, b_tabs[T_D]], writes=[b_tabs[T_W]])
                S.op("act", lambda: A.activation(out=tab(T_W), in_=tab(T_W), func=AF.Exp), reads=[b_tabs[T_W]], writes=[b_tabs[T_W]])
                S.op("dve", lambda: V.tensor_scalar(out=tab(T_W16), in0=tab(T_W), scalar1=1.0 / 16, scalar2=None, op0=ALU.mult),
                     reads=[b_tabs[T_W]], writes=[b_tabs[T_W16]])
                S.op("dve", lambda: V.tensor_tensor(out=tab(T_FL), in0=tab(T_B), in1=pbc[:, 0:96], op=ALU.add),
                     reads=[bpbc, b_tabs[T_B]], writes=[b_tabs[T_FL]])
                S.op("act", lambda: A.activation(out=tab(T_FL), in_=tab(T_FL), func=AF.Exp, scale=-1.0), reads=[b_tabs[T_FL]], writes=[b_tabs[T_FL]])

                with ExitStack() as ml:
                    qT = sb("qT", [128, 2, NTOK], BF16, stack=ml); b_q = Buf("q")
                    kT = sb("kT", [128, 2, NTOK], BF16, stack=ml); b_k = Buf("k")
                    ktok = sb("ktok", [128, NCH, 256], BF16, stack=ml); b_kt = Buf("kt")
                    vaug = sb("vaug", [128, NCH, 258], BF16, stack=ml); b_v = Buf("v")
                    hsum = sb("hsum", [128, NCH, 256], stack=ml); b_hs = [Buf("hs%d" % c) for c in range(NCH)]
                    bkv = sb("bkv", [128, 1024], stack=ml); b_bkv = Buf("bkv")
                    bkv2 = bkv[:, :].rearrange("p (t a n) -> p t a n", t=2, a=2)
                    NCHAIN = 4
                    Cst = [sb("Cst%d" % d, [128, 2, 257], stack=ml) for d in range(NCHAIN)]; b_Cst = [Buf("Cst%d" % d) for d in range(NCHAIN)]
                    Cb = [sb("Cb%d" % i, [128, 2, 258], BF16, stack=ml) for i in range(NCHAIN)]; b_Cb = [Buf("Cb%d" % i) for i in range(NCHAIN)]
                    Cstage = [sb("Cstage%d" % i, [128, 2, 256], stack=ml) for i in range(2)]; b_Cstage = [Buf("Cstage0"), Buf("Cstage1")]
                    STt = [sb("ST%d" % i, [128, 128], BF16, stack=ml) for i in range(NCHAIN)]; b_ST = [Buf("ST%d" % i) for i in range(NCHAIN)]
                    kw = [sb("kw%d" % i, [128, 256], BF16, stack=ml) for i in range(NCHAIN)]; b_kw = [Buf("kw%d" % i) for i in range(NCHAIN)]
                    hn = [sb("hn%d" % i, [128, 256], BF16, stack=ml) for i in range(2)]; b_hn = [Buf("hn0"), Buf("hn1")]
                    sgo = [sb("sgo%d" % i, [128, 512], BF16, stack=ml) for i in range(2)]; b_sgo = [Buf("sgo0"), Buf("sgo1")]
                    tiny = sb("tiny", [128, 8], stack=ml); b_tiny = [Buf("tiny%d" % i) for i in range(8)]
                    stats = sb("stats", [128, NCH, 6], stack=ml); b_stats = Buf("stats")
                    mv = sb("mv", [128, NCH, 2], stack=ml); b_mv = Buf("mv")
                    S.op("dve", lambda: V.memset(vaug[:, :, 256:258], 1.0), writes=[b_v])
                    for d_ in range(NCHAIN):
                        S.op("dve", lambda: V.memset(Cst[d_][:], 0.0), writes=[b_Cst[d_]])
                    cnt = {"c": 0, "stg": 0}
                    chains = [dict(d=0, chunks=list(range(0, 8)), load=True), dict(d=1, chunks=list(range(7, -1, -1)), load=True),
                              dict(d=0, chunks=list(range(8, 12)), load=False), dict(d=1, chunks=list(range(11, 7, -1)), load=False)]
                    for h in range(H):
                        S.dma("sp", bkv[:], bkv_bc[l, h], bkv_tl, writes=[b_bkv])
                        r, rb = wnext(); r3 = rv(r)
                        for blk in range(3):
                            for ec in range(2):
                                pb, bpb = proj_fm(l, lambda k: r3[:, k, ec * 128:(ec + 1) * 128], uT, [rb, b_u[blk]], blk)
                                S.op("act", lambda: A.activation(out=qT[:, ec, blk * 512:(blk + 1) * 512], in_=pb[:, :], func=AF.Identity,
                                                                 bias=bcol(l, "q", h * 2 + ec)), reads=[bpb, b_colp], writes=[b_q])
                                pb, bpb = proj_fm(l, lambda k: r3[:, k, 256 + ec * 128:256 + (ec + 1) * 128], uT, [rb, b_u[blk]], blk)
                                S.op("act", lambda: A.activation(out=kT[:, ec, blk * 512:(blk + 1) * 512], in_=pb[:, :], func=AF.Identity,
                                                                 bias=kb16[:, l, h * 2 + ec:h * 2 + ec + 1], scale=1.0 / 16),
                                     reads=[bpb, b_kb16], writes=[b_k])
                        for c2 in range(NCH // 2):
                            pb, bpb = PB()
                            for cc in range(2):
                                c = c2 * 2 + cc
                                for k in range(8):
                                    S.op("pe", lambda: PE.matmul(pb[:, cc * 256:(cc + 1) * 256], lhsT=uT[:, k, c * 128:(c + 1) * 128], rhs=r3[:, k, 256:512],
                                                                 start=(k == 0), stop=(k == 7)), reads=[rb, b_u[c // 4]], writes=[bpb], mark=(k == 7))
                            S.op("dve", lambda: V.tensor_tensor(out=ktok[:, c2 * 2:c2 * 2 + 2, :], in0=pb[:, :].rearrange("p (a n) -> p a n", a=2),
                                                                in1=bkv2[:, 0, :, :], op=ALU.add),
                                 reads=[bpb, b_bkv], writes=[b_kt])
                        r2, rb2 = wnext(); r23 = rv(r2)
                        for c2 in range(NCH // 2):
                            pb, bpb = PB()
                            for cc in range(2):
                                c = c2 * 2 + cc
                                for k in range(8):
                                    S.op("pe", lambda: PE.matmul(pb[:, cc * 256:(cc + 1) * 256], lhsT=uT[:, k, c * 128:(c + 1) * 128], rhs=r23[:, k, 0:256],
                                                                 start=(k == 0), stop=(k == 7)), reads=[rb2, b_u[c // 4]], writes=[bpb], mark=(k == 7))
                            S.op("dve", lambda: V.tensor_tensor(out=vaug[:, c2 * 2:c2 * 2 + 2, 0:256], in0=pb[:, :].rearrange("p (a n) -> p a n", a=2),
                                                                in1=bkv2[:, 1, :, :], op=ALU.add),
                                 reads=[bpb, b_bkv], writes=[b_v])
                        hs_written = [False] * NCH
                        for ci, ch in enumerate(chains):
                            if ch["load"]:
                                d = ch["d"]
                                S.dma("sp", Cst[ci][:, :, 0:256], C0[l, d, h].rearrange("(k p) e -> p k e", p=128), st_tls[ci], writes=[b_Cst[ci]])
                                S.dma("sp", Cst[ci][:, :, 256:257], n0T[l, :, :, d * 4 + h:d * 4 + h + 1], st_tls[ci], writes=[b_Cst[ci]],
                                      allow_slow_non_contiguous=True)
                        for k_, grp in [(k__, g__) for k__ in range(8) for g__ in ((0, 1), (2, 3))]:
                            act = [(ci, chains[ci], chains[ci]["chunks"][k_]) for ci in grp if k_ < len(chains[ci]["chunks"])]
                            if not act:
                                continue
                            pq, bpq = PB()
                            pns = [PB() for _ in act]
                            for ai, (ci, ch, c) in enumerate(act):
                                csl = slice(c * 128, (c + 1) * 128)
                                for dk in range(2):
                                    S.op("pe", lambda: PE.matmul(pq[:, ai * 128:(ai + 1) * 128], lhsT=kT[:, dk, csl], rhs=qT[:, dk, csl], start=(dk == 0), stop=(dk == 1)),
                                         reads=[b_q, b_k], writes=[bpq], mark=(dk == 1))
                            for ai, (ci, ch, c) in enumerate(act):
                                col = c * 8 + ch["d"] * 4 + h
                                maskT = TRIFb if ch["d"] == 0 else TRIBb
                                S.op("dve", lambda: V.scalar_tensor_tensor(out=STt[ci][:], in0=pq[:, ai * 128:(ai + 1) * 128], scalar=tabs[:, T_W, col:col + 1], in1=maskT,
                                                                           op0=ALU.mult, op1=ALU.mult), reads=[bpq, b_tabs[T_W], b_cmb], writes=[b_ST[ci]])
                                if ci % 2 == 0:
                                    S.op("act", lambda: A.activation(out=kw[ci][:], in_=ktok[:, c, :], func=AF.Copy, scale=tabs[:, T_W16, col:col + 1]),
                                         reads=[b_kt, b_tabs[T_W16]], writes=[b_kw[ci]])
                                else:
                                    S.op("dve", lambda: V.tensor_scalar(out=kw[ci][:], in0=ktok[:, c, :], scalar1=tabs[:, T_W16, col:col + 1], scalar2=None,
                                                                        op0=ALU.mult), reads=[b_kt, b_tabs[T_W16]], writes=[b_kw[ci]])
                            pus = []
                            for ai, (ci, ch, c) in enumerate(act):
                                pu, bpu = PB()
                                for dk in range(2):
                                    S.op("pe", lambda: PE.matmul(pu[:, dk * 256:(dk + 1) * 256], lhsT=kw[ci][:, dk * 128:(dk + 1) * 128], rhs=vaug[:, c, 0:256], start=True, stop=True),
                                         reads=[b_kw[ci], b_v], writes=[bpu], mark=False)
                                    S.op("pe", lambda: PE.matmul(pns[ai][0][:, 384 + dk:385 + dk], lhsT=kw[ci][:, dk * 128:(dk + 1) * 128], rhs=vaug[:, c, 256:257], start=True, stop=True),
                                         reads=[b_kw[ci], b_v], writes=[bpu, pns[ai][1]], mark=(dk == 1))
                                pus.append((pu, bpu))
                            for ci, ch, c in act:
                                col = c * 8 + ch["d"] * 4 + h
                                S.op("act", lambda: A.activation(out=Cb[ci][:, :, 0:257], in_=Cst[ci][:], func=AF.Copy, scale=tabs[:, T_A, col:col + 1]),
                                     reads=[b_Cst[ci], b_tabs[T_A]], writes=[b_Cb[ci]])
                            for ai, (ci, ch, c) in enumerate(act):
                                csl = slice(c * 128, (c + 1) * 128)
                                pn, bpn = pns[ai]
                                for dk in range(2):
                                    S.op("pe", lambda: PE.matmul(pn[:, 0:257], lhsT=qT[:, dk, csl], rhs=Cb[ci][:, dk, 0:257], start=(dk == 0), stop=False),
                                         reads=[b_q, b_Cb[ci]], writes=[bpn], mark=False)
                                S.op("pe", lambda: PE.matmul(pn[:, 0:257], lhsT=STt[ci][:], rhs=vaug[:, c, 0:257], start=False, stop=True),
                                     reads=[b_ST[ci], b_v], writes=[bpn], mark=True)
                            for ai, (ci, ch, c) in enumerate(act):
                                pu, bpu = pus[ai]
                                col = c * 8 + ch["d"] * 4 + h
                                S.op("dve", lambda: V.scalar_tensor_tensor(out=Cst[ci][:, :, 0:256], in0=Cst[ci][:, :, 0:256], scalar=tabs[:, T_A, col:col + 1],
                                                                           in1=pu[:, :].rearrange("p (a n) -> p a n", a=2), op0=ALU.mult, op1=ALU.add),
                                     reads=[bpu, b_Cst[ci], b_tabs[T_A]], writes=[b_Cst[ci]])
                                S.op("dve", lambda: V.scalar_tensor_tensor(out=Cst[ci][:, :, 256:257], in0=Cst[ci][:, :, 256:257], scalar=tabs[:, T_A, col:col + 1],
                                                                           in1=pns[ai][0][:, 384:386].rearrange("p (a n) -> p a n", n=1), op0=ALU.mult, op1=ALU.add),
                                     reads=[pns[ai][1], b_Cst[ci], b_tabs[T_A]], writes=[b_Cst[ci]])
                            for ai, (ci, ch, c) in enumerate(act):
                                d = ch["d"]
                                col = c * 8 + d * 4 + h
                                pn, bpn = pns[ai]
                                it = cnt["c"] % 8
                                cnt["c"] += 1
                                tn = tiny[:, it:it + 1]
                                S.op("dve", lambda: V.tensor_tensor(out=tn, in0=pn[:, 256:257], in1=tabs[:, T_FL, col:col + 1], op=ALU.max),
                                     reads=[bpn, b_tabs[T_FL]], writes=[b_tiny[it]])
                                S.op("dve", lambda: V.scalar_tensor_tensor(out=tn, in0=pn[:, 256:257], scalar=-1.0, in1=tn, op0=ALU.mult, op1=ALU.max),
                                     reads=[bpn, b_tiny[it]], writes=[b_tiny[it]])
                                S.op("dve", lambda: V.reciprocal(out=tn, in_=tn), reads=[b_tiny[it]], writes=[b_tiny[it]])
                                if not hs_written[c]:
                                    hs_written[c] = True
                                    S.op("dve", lambda: V.tensor_scalar(out=hsum[:, c, :], in0=pn[:, 0:256], scalar1=tn, scalar2=None, op0=ALU.mult),
                                         reads=[bpn, b_tiny[it]], writes=[b_hs[c]])
                                else:
                                    S.op("dve", lambda: V.scalar_tensor_tensor(out=hsum[:, c, :], in0=pn[:, 0:256], scalar=tn, in1=hsum[:, c, :],
                                                                               op0=ALU.mult, op1=ALU.add), reads=[bpn, b_tiny[it], b_hs[c]], writes=[b_hs[c]])
                                seg_end = (c % 2 == 1) if d == 0 else (c % 2 == 0)
                                if seg_end:
                                    seg = c // 2
                                    idx = seg * 8 + d * 4 + h
                                    sg_ = cnt["stg"] % 2
                                    cnt["stg"] += 1
                                    S.op("act", lambda: A.activation(out=Cstage[sg_][:], in_=Cst[ci][:, :, 0:256], func=AF.Copy), reads=[b_Cst[ci]], writes=[b_Cstage[sg_]])
                                    S.op("dve", lambda: V.tensor_copy(out=nall[:, :, idx:idx + 1], in_=Cst[ci][:, :, 256:257]), reads=[b_Cst[ci]], writes=[b_nall])
                                    S.dma("sp", oC_d[seg, l, d, h].rearrange("(k p) e -> p k e", p=128), Cstage[sg_][:], oc_tls[sg_], reads=[b_Cstage[sg_]])
                        for c in range(NCH):
                            S.op("dve", lambda: V.bn_stats(out=stats[:, c, :], in_=hsum[:, c, :]), reads=[b_hs[c]], writes=[b_stats])
                        for c in range(NCH):
                            S.op("dve", lambda: V.bn_aggr(out=mv[:, c, :], in_=stats[:, c, :]), reads=[b_stats], writes=[b_mv])
                        S.op("act", lambda: A.activation(out=mv[:, :, 1:2], in_=mv[:, :, 1:2], func=AF.Sqrt, bias=cst[:, 1:2]), reads=[b_mv, b_ones], writes=[b_mv])
                        S.op("dve", lambda: V.reciprocal(out=mv[:, :, 1:2], in_=mv[:, :, 1:2]), reads=[b_mv], writes=[b_mv])
                        for blk in range(3):
                            for ec in range(2):
                                pb, bpb = proj_fm(l, lambda k: r23[:, k, 256 + ec * 128:256 + (ec + 1) * 128], uT, [rb2, b_u[blk]], blk)
                                S.op("act", lambda: A.activation(out=sgo[ec][:], in_=pb[:, :], func=AF.Sigmoid, bias=bcol(l, "o", h * 2 + ec)),
                                     reads=[bpb, b_colp], writes=[b_sgo[ec]])
                            for cc in range(4):
                                c = blk * 4 + cc
                                i2 = c % 2
                                S.op("dve", lambda: V.tensor_scalar(out=hn[i2][:], in0=hsum[:, c, :], scalar1=mv[:, c, 0:1], scalar2=mv[:, c, 1:2],
                                                                    op0=ALU.subtract, op1=ALU.mult), reads=[b_hs[c], b_mv], writes=[b_hn[i2]])
                                pt, bpt = PT()
                                for ec in range(2):
                                    S.op("pe", lambda: PE.transpose(out=pt[:, ec * 128:(ec + 1) * 128], in_=hn[i2][:, ec * 128:(ec + 1) * 128], identity=IDNb),
                                         reads=[b_hn[i2], b_cmb], writes=[bpt], mark=(ec == 1))
                                for ec in range(2):
                                    S.op("dve", lambda: V.scalar_tensor_tensor(out=brT[:, h * 2 + ec, c * 128:(c + 1) * 128], in0=pt[:, ec * 128:(ec + 1) * 128],
                                                                               scalar=colp[:, l, C_MNG + h * 2 + ec:C_MNG + h * 2 + ec + 1],
                                                                               in1=sgo[ec][:, cc * 128:(cc + 1) * 128], op0=ALU.mult, op1=ALU.mult),
                                         reads=[bpt, b_colp, b_sgo[ec]], writes=[b_br[blk]])
                    S.dma("sp", on_d[l], nall[:, :, :].rearrange("p a b -> p (a b)"), out_tl, reads=[b_nall])
                    S.barrier()

                merged = sb("merged", [128, 8, NTOK], stack=mix); b_mg = [Buf("mg%d" % i) for i in range(3)]

                def branch_merge(gname, first):
                    with ExitStack() as ph:
                        sg = [sb("sg%d" % i, [128, 512], stack=ph) for i in range(2)]; b_sg = [Buf("sg0"), Buf("sg1")]
                        n = 0
                        for jq in range(4):
                            rw, rwb = wnext(); rw3 = rv(rw)
                            rgb = rwb
                            for q in range(2):
                                j = jq * 2 + q
                                for blk in range(3):
                                    sl = slice(blk * 512, (blk + 1) * 512)
                                    i2 = n % 2
                                    n += 1
                                    pg_, bpg_ = proj_fm(l, lambda k: rw3[:, k, 256 + q * 128:256 + (q + 1) * 128], uT, [rgb, b_u[blk]], blk)
                                    S.op("act", lambda: A.activation(out=sg[i2][:], in_=pg_[:, :], func=AF.Sigmoid, bias=bcol(l, gname, j)),
                                         reads=[bpg_, b_colp], writes=[b_sg[i2]])
                                    pp, bpp = proj_fm(l, lambda k: rw3[:, k, q * 128:(q + 1) * 128], brT, [rwb, b_br[blk]], blk)
                                    if first:
                                        S.op("dve", lambda: V.tensor_tensor(out=merged[:, j, sl], in0=pp[:, :], in1=sg[i2][:], op=ALU.mult),
                                             reads=[bpp, b_sg[i2]], writes=[b_mg[blk]])
                                    else:
                                        S.op("dve", lambda: V.tensor_tensor(out=sg[i2][:], in0=pp[:, :], in1=sg[i2][:], op=ALU.mult),
                                             reads=[bpp, b_sg[i2]], writes=[b_sg[i2]])
                                        S.op("dve", lambda: V.tensor_tensor(out=merged[:, j, sl], in0=merged[:, j, sl], in1=sg[i2][:], op=ALU.add),
                                             reads=[b_sg[i2], b_mg[blk]], writes=[b_mg[blk]])
                        S.barrier()
                branch_merge("gm", True)

                with ExitStack() as ft:
                    xf = sb("xf", [128, 2, NTOK], BF16, stack=ft); b_xf = Buf("xf")
                    Y = sb("Y", [128, NCH, 512], BF16, stack=ft); b_Y = Buf("Y")
                    for g in range(4):
                        if True:
                            r, rb = wnext(); r3 = rv(r, 8, 256)
                            gg = 0
                            for blk in range(3):
                                for ec in range(2):
                                    pb, bpb = proj_fm(l, lambda k: r3[:, k, gg * 256 + ec * 128:gg * 256 + (ec + 1) * 128], uT, [rb, b_u[blk]], blk)
                                    S.op("act", lambda: A.activation(out=xf[:, ec, blk * 512:(blk + 1) * 512], in_=pb[:, :], func=AF.Identity,
                                                                     bias=bcol(l, "f", g * 2 + ec)), reads=[bpb, b_colp], writes=[b_xf])
                            for c in range(NCH):
                                pb, bpb = PB()
                                for ec in range(2):
                                    S.op("pe", lambda: PE.matmul(pb[:, :], lhsT=xf[:, ec, c * 128:(c + 1) * 128], rhs=dcs[:, ec, :], start=(ec == 0), stop=(ec == 1)),
                                         reads=[b_xf, b_dcs], writes=[bpb], mark=(ec == 1))
                                if c % 2 == 0:
                                    S.op("act", lambda: A.activation(out=Y[:, c, :], in_=pb[:, :], func=AF.Copy), reads=[bpb], writes=[b_Y])
                                else:
                                    S.op("dve", lambda: V.tensor_copy(out=Y[:, c, :], in_=pb[:, :]), reads=[bpb], writes=[b_Y])
                            for tq in range(4):
                                rc, rcb = wnext(); rc3 = rv(rc)
                                pb, bpb = PB()
                                for ec in range(2):
                                    for tc_ in range(8):
                                        S.op("pe", lambda: PE.matmul(pb[:, ec * 256:(ec + 1) * 256], lhsT=Y[:, tc_, ec * 128:(ec + 1) * 128], rhs=rc3[:, tc_, 0:256], start=(tc_ == 0), stop=False),
                                             reads=[b_Y, rcb], writes=[bpb], mark=False)
                                        S.op("pe", lambda: PE.matmul(pb[:, ec * 256:(ec + 1) * 256], lhsT=Y[:, tc_, 256 + ec * 128:256 + (ec + 1) * 128], rhs=rc3[:, tc_, 256:512], start=False, stop=(tc_ == 7)),
                                             reads=[b_Y, rcb], writes=[bpb], mark=(tc_ == 7))
                                S.op("act", lambda: A.activation(out=brT[:, g * 2:g * 2 + 2, tq * 256:(tq + 1) * 256], in_=pb[:, :].rearrange("p (a n) -> p a n", a=2), func=AF.Copy),
                                     reads=[bpb], writes=[b_br[tq // 2]])
                            for ec in range(2):
                                pb, bpb = PB()
                                for sg_ in range(2):
                                    for tc_ in range(2):
                                        c = 8 + sg_ * 2 + tc_
                                        S.op("pe", lambda: PE.matmul(pb[:, sg_ * 256:(sg_ + 1) * 256], lhsT=Y[:, c, ec * 128:(ec + 1) * 128], rhs=tB[:, tc_, 0, :],
                                                                     start=(tc_ == 0), stop=False), reads=[b_Y, b_tB], writes=[bpb], mark=False)
                                        S.op("pe", lambda: PE.matmul(pb[:, sg_ * 256:(sg_ + 1) * 256], lhsT=Y[:, c, 256 + ec * 128:256 + (ec + 1) * 128], rhs=tB[:, tc_, 1, :],
                                                                     start=False, stop=(tc_ == 1)), reads=[b_Y, b_tB], writes=[bpb], mark=(tc_ == 1))
                                S.op("dve", lambda: V.tensor_copy(out=brT[:, g * 2 + ec, 1024:1536], in_=pb[:, :]), reads=[bpb], writes=[b_br[2]])
                    S.barrier()
                branch_merge("gf", False)

                with ExitStack() as cv:
                    cmask = sb("cmask", [128, 2, NTOK], BF16, stack=cv); b_cmk = Buf("cmask")
                    S.barrier(("pool",))
                    S.dma("pool", cmask[:], cmask_d.rearrange("p (a t) -> p a t", a=2), cmask_tl, writes=[b_cmk])
                    uc = sb("uc", [128, NTOK + 2], stack=cv); b_uc = Buf("uc")
                    ccs = [sb("ccs%d" % i, [128, 512], stack=cv) for i in range(2)]; b_ccs = [Buf("ccs0"), Buf("ccs1")]
                    t1 = [sb("t1%d" % i, [128, 512], stack=cv) for i in range(2)]; b_t1 = [Buf("t10"), Buf("t11")]
                    t2 = [sb("t2%d" % i, [128, 8], stack=cv) for i in range(2)]; b_t2 = [Buf("t20"), Buf("t21")]
                    ncwt = sb("ncwt", [128, 24], stack=cv); b_ncw = Buf("ncw")
                    S.op("dve", lambda: V.tensor_scalar(out=ncwt[:], in0=colp[:, l, C_W0:C_W0 + 24], scalar1=-1.0, scalar2=None, op0=ALU.mult),
                         reads=[b_colp], writes=[b_ncw])
                    S.op("dve", lambda: V.memset(uc[:], 0.0), writes=[b_uc])
                    n = 0
                    for j in range(8):
                        r, rb = wnext(); r3 = rv(r, 8, 384)
                        pcb = []
                        for blk in range(3):
                            sl = slice(blk * 512, (blk + 1) * 512)
                            i2 = n % 2
                            n += 1
                            pc_, bpc_ = proj_fm(l, lambda k: r3[:, k, 0:128], uT, [rb, b_u[blk]], blk)
                            S.op("act", lambda: A.activation(out=ccs[i2][:], in_=pc_[:, :], func=AF.Identity, bias=bcol(l, "cc", j)),
                                 reads=[bpc_, b_colp], writes=[b_ccs[i2]])
                            px_, bpx_ = proj_fm(l, lambda k: r3[:, k, 128:256], uT, [rb, b_u[blk]], blk)
                            S.op("dve", lambda: V.scalar_tensor_tensor(out=uc[:, 1 + blk * 512:1 + (blk + 1) * 512], in0=px_[:, :], scalar=bcol(l, "cx", j), in1=ccs[i2][:],
                                                                       op0=ALU.add, op1=ALU.mult), reads=[bpx_, b_colp, b_ccs[i2]], writes=[b_uc])
                        for blk in range(3):
                            sl = slice(blk * 512, (blk + 1) * 512)
                            i2 = n % 2
                            n += 1
                            pb_, bpb_ = proj_fm(l, lambda k: r3[:, k, 256:384], uT, [rb, b_u[blk]], blk)
                            cw = lambda q: colp[:, l, q + j:q + j + 1]
                            S.op("act", lambda: A.activation(out=t1[i2][:], in_=uc[:, 1 + blk * 512:1 + (blk + 1) * 512], func=AF.Identity,
                                                             scale=cw(C_W1), bias=cw(C_CB)), reads=[b_uc, b_colp], writes=[b_t1[i2]])
                            Lv = uc[:, blk * 512:(blk + 1) * 512]
                            Rv = uc[:, 2 + blk * 512:2 + (blk + 1) * 512]
                            S.op("dve", lambda: V.scalar_tensor_tensor(out=t1[i2][:], in0=Lv, scalar=cw(C_W0), in1=t1[i2][:], op0=ALU.mult, op1=ALU.add),
                                 reads=[b_uc, b_t1[i2], b_colp], writes=[b_t1[i2]])
                            S.op("dve", lambda: V.scalar_tensor_tensor(out=t1[i2][:], in0=Rv, scalar=cw(C_W2), in1=t1[i2][:], op0=ALU.mult, op1=ALU.add),
                                 reads=[b_uc, b_t1[i2], b_colp], writes=[b_t1[i2]])
                            st64 = lambda ap, o: ap.rearrange("p (a b) -> p a b", b=64)[:, :, o]
                            ncw = lambda q: ncwt[:, q - C_W0 + j:q - C_W0 + j + 1]
                            S.op("dve", lambda: V.tensor_tensor(out=t2[0][:, 0:8], in0=st64(Lv, 0), in1=st64(cmask[:, 0, sl], 0), op=ALU.mult),
                                 reads=[b_uc, b_cmk], writes=[b_t2[0]])
                            S.op("dve", lambda: V.scalar_tensor_tensor(out=st64(t1[i2][:], 0), in0=t2[0][:, 0:8], scalar=ncw(C_W0), in1=st64(t1[i2][:], 0),
                                                                       op0=ALU.mult, op1=ALU.add), reads=[b_t2[0], b_t1[i2], b_ncw], writes=[b_t1[i2]])
                            S.op("dve", lambda: V.tensor_tensor(out=t2[1][:, 0:8], in0=st64(Rv, 63), in1=st64(cmask[:, 1, sl], 63), op=ALU.mult),
                                 reads=[b_uc, b_cmk], writes=[b_t2[1]])
                            S.op("dve", lambda: V.scalar_tensor_tensor(out=st64(t1[i2][:], 63), in0=t2[1][:, 0:8], scalar=ncw(C_W2), in1=st64(t1[i2][:], 63),
                                                                       op0=ALU.mult, op1=ALU.add), reads=[b_t2[1], b_t1[i2], b_ncw], writes=[b_t1[i2]])
                            S.op("dve", lambda: V.scalar_tensor_tensor(out=brT[:, j, sl], in0=pb_[:, :], scalar=bcol(l, "cb", j), in1=t1[i2][:],
                                                                       op0=ALU.add, op1=ALU.mult), reads=[bpb_, b_colp, b_t1[i2]], writes=[b_br[blk]])
                    S.barrier()
                branch_merge("gc", False)
                for j in range(8):
                    for blk in range(3):
                        sl = slice(blk * 512, (blk + 1) * 512)
                        S.op("act", lambda: A.activation(out=brT[:, j, sl], in_=merged[:, j, sl], func=AF.Copy), reads=[b_mg[blk]], writes=[b_br[blk]])
                ws = {}

                def wout_fn(j):
                    if j % 4 == 0:
                        ws["r"], ws["b"] = wnext()
                    r3 = rv(ws["r"])
                    q = j % 4
                    return (lambda k: r3[:, k, q * 128:(q + 1) * 128], ws["b"])
                residual_ln(0, wout_fn, brT, b_br, 8)

            modulate(3, 4)
            with ExitStack() as ff:
                gT = sb("gT", [128, NFC, NTOK], BF16, stack=ff); b_g = [Buf("g%d" % i) for i in range(3)]
                sa = [sb("sa%d" % i, [128, 512], stack=ff) for i in range(2)]; b_sa = [Buf("sa0"), Buf("sa1")]
                n = 0
                for f2 in range(11):
                    r, rb = wnext(); r3 = rv(r)
                    for q in range(2):
                        f = f2 * 2 + q
                        for blk in range(3):
                            sl = slice(blk * 512, (blk + 1) * 512)
                            i2 = n % 2
                            n += 1
                            pa, bpa = proj_fm(l, lambda k: r3[:, k, q * 128:(q + 1) * 128], uT, [rb, b_u[blk]], blk)
                            S.op("act", lambda: A.activation(out=sa[i2][:], in_=pa[:, :], func=AF.Silu), reads=[bpa], writes=[b_sa[i2]])
                            pb2, bpb2 = proj_fm(l, lambda k: r3[:, k, 256 + q * 128:256 + (q + 1) * 128], uT, [rb, b_u[blk]], blk)
                            S.op("dve", lambda: V.tensor_tensor(out=gT[:, f, sl], in0=pb2[:, :], in1=sa[i2][:], op=ALU.mult),
                                 reads=[bpb2, b_sa[i2]], writes=[b_g[blk]])
                    if l + 1 < DEPTH:
                        ada_tile(l + 1, f2)

                def wdn_fn(j):
                    if j == 1 and l + 1 < DEPTH:
                        ada_tile(l + 1, 11)
                        ada_finish(l + 1)
                    r, rb = wnext()
                    r3 = rv(r, K=NFC, width=128)
                    return (lambda k: r3[:, k, :], rb)
                residual_ln(1, wdn_fn, gT, b_g, NFC)

        with ExitStack() as ph:
            ys = [sb("ys%d" % i, [128, D], stack=ph) for i in range(2)]; b_ys = [Buf("ys0"), Buf("ys1")]
            for c in range(NCH):
                i = c % 2
                for k2 in range(2):
                    pb, bpb = PB()
                    for kk in range(4):
                        k = k2 * 4 + kk
                        S.op("pe", lambda: PE.transpose(out=pb[:, kk * 128:(kk + 1) * 128], in_=xT[:, k, c * 128:(c + 1) * 128], identity=IDN),
                             reads=[b_x[c // 4], b_cmat], writes=[bpb], mark=(kk == 3))
                    if k2 == 0:
                        S.op("act", lambda: A.activation(out=ys[i][:, 0:512], in_=pb[:, :], func=AF.Copy), reads=[bpb], writes=[b_ys[i]])
                    else:
                        S.op("dve", lambda: V.tensor_copy(out=ys[i][:, 512:1024], in_=pb[:, :]), reads=[bpb], writes=[b_ys[i]])
                S.dma("sp", y_d[c * 128:(c + 1) * 128, :], ys[i][:], ys_tls[i], reads=[b_ys[i]])
            S.final_wait("sp")
        assert wq["n"] == len(wq["plan"]), (wq["n"], len(wq["plan"]))
    return nc


_CACHE = {}


def _consts():
    if "c" in _CACHE:
        return _CACHE["c"]
    i = np.arange(128)
    trif = (i[:, None] <= i[None, :]).astype(np.float32)
    trib = (i[:, None] >= i[None, :]).astype(np.float32)
    cmat = np.concatenate([trif, trib, np.eye(128, dtype=np.float32), np.full((128, 128), 1.0 / 1024, np.float32)], axis=1)
    c256 = np.arange(256, dtype=np.float64)
    ang = 2 * np.pi * np.outer(c256, c256) / 256.0
    dcs = np.concatenate([np.cos(ang), np.sin(ang)], axis=1).astype(np.float32)

    def tmat(T, scale):
        t = np.arange(T, dtype=np.float64)
        a = 2 * np.pi * (np.outer(t, t) % T) / T
        return np.cos(a) * scale, -np.sin(a) * scale
    cs, ns = tmat(1024, 1.0 / np.sqrt(1024 * 256.0))
    tA_s = np.stack([cs, ns]).astype(np.float32)
    cp, npn = tmat(256, 1.0 / 256.0)
    tB = np.stack([cp, npn]).astype(np.float32)
    tA_p = np.zeros((2, 1024, 1024), np.float32)
    for s in range(4):
        tA_p[0, s * 256:(s + 1) * 256, s * 256:(s + 1) * 256] = cp
        tA_p[1, s * 256:(s + 1) * 256, s * 256:(s + 1) * 256] = npn
    quarter = D // 4
    freqs = (10000.0 ** (-np.arange(quarter, dtype=np.float32) / np.float32(quarter))).astype(np.float32)
    rows = 16
    r = np.repeat(np.arange(rows, dtype=np.float32), 64)[:, None] * freqs
    col = np.tile(np.arange(64, dtype=np.float32), rows)[:, None] * freqs
    pos = np.concatenate([np.sin(r), np.cos(r), np.sin(col), np.cos(col)], axis=-1).astype(np.float32)
    t = np.arange(NTOK)
    def masks(rowlenA):
        rl = np.where(t < 1024, rowlenA, 256)
        ml = ((t % rl) == 0).astype(np.float32)
        mr = ((t % rl) == rl - 1).astype(np.float32)
        return np.tile(np.concatenate([ml, mr])[None, :], (128, 1)).astype(np.float32)
    def keep(kA):
        k = np.zeros((12, 2, 4), np.float32)
        fw = [1, 1, kA, 1, kA, 1, kA, 1, 0, 1, 0, 1]
        bw = [1, kA, 1, kA, 1, kA, 1, 1, 1, 0, 1, 0]
        for c in range(12):
            k[c, 0, :] = fw[c]
            k[c, 1, :] = bw[c]
        return k.reshape(1, 96)
    c = dict(cmat=cmat, dcs=dcs, tA_s=tA_s, tA_p=tA_p, tB=tB, pos=pos, pos0=np.zeros_like(pos),
             cmask_s=masks(64), cmask_p=masks(256), keep_s=keep(1.0), keep_p=keep(0.0))
    _CACHE["c"] = c
    return c


def kernel(x_prompt, x_sample, state_C, state_n, state_m, c, c_ctx, w_ada, b_ada, w_in, b_in,
           mlstm_norm_g, conv_w, conv_b, w_br_mlstm, w_br_fourier, w_br_conv, w_out, ln_g, ln_b,
           w_gate_up, w_down):
    f32 = np.float32
    K = _consts()
    A = lambda a: np.ascontiguousarray(np.asarray(a, dtype=f32))
    x_prompt, x_sample = A(x_prompt), A(x_sample)
    state_C, state_n, state_m = A(state_C), A(state_n), A(state_m)
    c, c_ctx = A(c), A(c_ctx)
    b_ada, b_in = A(b_ada), A(b_in)
    Wd = {"w_ada": A(w_ada), "w_in": A(w_in), "w_br_mlstm": A(w_br_mlstm), "w_br_fourier": A(w_br_fourier),
          "w_br_conv": A(w_br_conv), "w_out": A(w_out), "w_gate_up": A(w_gate_up), "w_down": A(w_down)}
    wpack_s = pack_weights(Wd)
    wpack_p = wpack_s
    if "tAp" not in _CACHE:
        _CACHE["tAp"] = (pack_tA(K["tA_s"]), pack_tA(K["tA_p"]))
    tAp_s, tAp_p = _CACHE["tAp"]
    shared = {"cmat": K["cmat"], "dcs": K["dcs"], "tB": K["tB"]}
    badaT = np.repeat(b_ada.reshape(DEPTH, 48, 128).transpose(0, 2, 1)[:, :, :, None], 2, axis=3).reshape(DEPTH, 128, 96)
    colp = np.zeros((DEPTH, 128, NCOL), f32)
    for l in range(DEPTH):
        for name, bi in BLK.items():
            off = BLK_OFF[name]
            colp[l, :, bi * 8:bi * 8 + 8] = b_in[l, off:off + 1024].reshape(8, 128).T
        colp[l, :, C_MNG:C_MNG + 8] = A(mlstm_norm_g)[l].reshape(8, 128).T
        cw = A(conv_w)
        colp[l, :, C_W0:C_W0 + 8] = cw[l, 0].reshape(8, 128).T
        colp[l, :, C_W1:C_W1 + 8] = cw[l, 1].reshape(8, 128).T
        colp[l, :, C_W2:C_W2 + 8] = cw[l, 2].reshape(8, 128).T
        colp[l, :, C_CB:C_CB + 8] = A(conv_b)[l].reshape(8, 128).T
        colp[l, :, C_LNG0:C_LNG0 + 8] = A(ln_g)[l, 0].reshape(8, 128).T
        colp[l, :, C_LNB0:C_LNB0 + 8] = A(ln_b)[l, 0].reshape(8, 128).T
        colp[l, :, C_LNG1:C_LNG1 + 8] = A(ln_g)[l, 1].reshape(8, 128).T
        colp[l, :, C_LNB1:C_LNB1 + 8] = A(ln_b)[l, 1].reshape(8, 128).T
    bg_bc = np.ascontiguousarray(np.broadcast_to(np.tile(b_in[:, G_OFF:G_OFF + 16], (1, 12))[:, None, :], (DEPTH, 128, 192)))
    bkv = np.zeros((DEPTH, H, 128, 4, 256), f32)
    for l in range(DEPTH):
        for h in range(H):
            bkv[l, h, :, 0:2, :] = b_in[l, K_OFF + h * 256:K_OFF + (h + 1) * 256][None, None, :]
            bkv[l, h, :, 2:4, :] = b_in[l, V_OFF + h * 256:V_OFF + (h + 1) * 256][None, None, :]
    bkv = bkv.reshape(DEPTH, H, 128, 1024)
    shared.update(badaT=np.ascontiguousarray(badaT), colp=colp, bg_bc=bg_bc, bkv_bc=bkv)

    in_maps = []
    prompts_of = []
    for core in range(8):
        m = dict(shared)
        if core < 4:
            pa = None
            pb_ = [2 * core, 2 * core + 1]
            xa = x_sample[core]
            condA = c[core]
            m["pos"] = K["pos"]; m["wpack"] = wpack_s; m["tApack"] = tAp_s; m["cmask"] = K["cmask_s"]; m["keep"] = K["keep_s"]
            m["C0"] = np.ascontiguousarray(state_C[core])
            n0 = state_n[core]
            m["m0"] = np.ascontiguousarray(state_m[core].reshape(DEPTH, 1, 8))
        else:
            base = 8 + 6 * (core - 4)
            pa = [base, base + 1, base + 2, base + 3]
            pb_ = [base + 4, base + 5]
            xa = x_prompt[pa].reshape(1024, D)
            condA = c_ctx
            m["pos"] = K["pos0"]; m["wpack"] = wpack_p; m["tApack"] = tAp_p; m["cmask"] = K["cmask_p"]; m["keep"] = K["keep_p"]
            m["C0"] = np.zeros((DEPTH, 2, H, DH, DH), f32)
            n0 = np.zeros((DEPTH, 2, H, DH), f32)
            m["m0"] = np.zeros((DEPTH, 1, 8), f32)
        m["n0T"] = np.ascontiguousarray(n0.reshape(DEPTH, 2, H, 2, 128).transpose(0, 4, 3, 1, 2).reshape(DEPTH, 128, 2, 8))
        m["xin"] = np.ascontiguousarray(np.concatenate([xa, x_prompt[pb_].reshape(512, D)], axis=0))
        m["condT"] = np.ascontiguousarray(np.stack([condA, c_ctx], axis=1))
        in_maps.append(m)
        prompts_of.append((pa, pb_))

    if "nc" not in _CACHE:
        _CACHE["nc"] = build_program()
    res = run_bass_kernel_spmd(_CACHE["nc"], in_maps, core_ids=list(range(8)))

    y_prompt = np.zeros((32, 256, D), f32)
    y_sample = np.zeros((4, 1024, D), f32)
    nC = np.zeros((32, DEPTH, 2, H, DH, DH), f32)
    nn = np.zeros((32, DEPTH, 2, H, DH), f32)
    nm = np.zeros((32, DEPTH, 2, H), f32)
    for core in range(8):
        r = res.results[core]
        y = r["y"]; oC = r["oC"]; on = r["on_raw"].reshape(DEPTH, 128, 2, 48); om = r["om_raw"].reshape(DEPTH, 12, 2, 4)
        pa, pb_ = prompts_of[core]
        segs = []
        if pa is None:
            y_sample[core] = y[0:1024]
        else:
            for s, p in enumerate(pa):
                y_prompt[p] = y[s * 256:(s + 1) * 256]
                segs.append((s, p))
        for s, p in enumerate(pb_):
            y_prompt[p] = y[1024 + s * 256:1024 + (s + 1) * 256]
            segs.append((4 + s, p))
        for s, p in segs:
            nC[p] = oC[s]
            blk = on[:, :, :, s * 8:(s + 1) * 8].reshape(DEPTH, 128, 2, 2, H)
            nn[p] = blk.transpose(0, 3, 4, 2, 1).reshape(DEPTH, 2, H, DH)
            nm[p, :, 0, :] = om[:, 2 * s + 1, 0, :]
            nm[p, :, 1, :] = om[:, 2 * s, 1, :]
    return (y_prompt, y_sample, nC, nn, nm)
```

```python
import numpy as np
from contextlib import ExitStack
import concourse.bass as bass
import concourse.mybir as mybir
from concourse.bass_utils import run_bass_kernel_spmd

F32 = mybir.dt.float32
BF16 = mybir.dt.bfloat16
F32R = mybir.dt.float32r
AF = mybir.ActivationFunctionType
ALU = mybir.AluOpType
AX = mybir.AxisListType

D = 1024
NTOK = 1536
NCH = 12
DEPTH = 2
H = 4
DH = 256
D_FF = 2816
NFC = 22
N_IN = 11280
Q_OFF, K_OFF, V_OFF, O_OFF, G_OFF, F_OFF = 0, 1024, 2048, 3072, 4096, 4112
CB_OFF, CC_OFF, CX_OFF, GM_OFF, GF_OFF, GC_OFF = 5136, 6160, 7184, 8208, 9232, 10256
ALPHA = (2 * DEPTH) ** 0.25
LN_EPS = 1e-5
BLK = {"q": 0, "k": 1, "o": 2, "f": 3, "cb": 4, "cc": 5, "cx": 6, "gm": 7, "gf": 8, "gc": 9}
BLK_OFF = {"q": Q_OFF, "k": K_OFF, "o": O_OFF, "f": F_OFF, "cb": CB_OFF, "cc": CC_OFF, "cx": CX_OFF,
           "gm": GM_OFF, "gf": GF_OFF, "gc": GC_OFF}
C_MNG, C_W0, C_W1, C_W2, C_CB, C_LNG0, C_LNB0, C_LNG1, C_LNB1, NCOL = 80, 88, 96, 104, 112, 120, 128, 136, 144, 152


def tile_specs():
    T = []
    for l in range(DEPTH):
        if l == 0:
            for t in range(12):
                T.append((8, 512, [("w_ada", l, t * 512, 512, 0)]))
        T.append((8, 16, [("w_in", l, G_OFF, 16, 0)]))
        for h in range(H):
            T.append((8, 512, [("w_in", l, Q_OFF + h * 256, 256, 0), ("w_in", l, K_OFF + h * 256, 256, 256)]))
            T.append((8, 512, [("w_in", l, V_OFF + h * 256, 256, 0), ("w_in", l, O_OFF + h * 256, 256, 256)]))

        def branch(wname, goff):
            for jq in range(4):
                T.append((8, 512, [(wname, l, jq * 256, 256, 0), ("w_in", l, goff + jq * 256, 256, 256)]))
        branch("w_br_mlstm", GM_OFF)
        for g in range(4):
            T.append((8, 256, [("w_in", l, F_OFF + g * 256, 256, 0)]))
            for tq in range(4):
                T.append((8, 512, [("tA", 0, tq * 256, 256, 0), ("tA", 1, tq * 256, 256, 256)]))
        branch("w_br_fourier", GF_OFF)
        for j in range(8):
            T.append((8, 384, [("w_in", l, CC_OFF + j * 128, 128, 0), ("w_in", l, CX_OFF + j * 128, 128, 128),
                               ("w_in", l, CB_OFF + j * 128, 128, 256)]))
        branch("w_br_conv", GC_OFF)
        for jj in range(2):
            T.append((8, 512, [("w_out", l, jj * 512, 512, 0)]))
        for f2 in range(11):
            T.append((8, 512, [("w_gate_up", l, f2 * 256, 256, 0), ("w_gate_up", l, D_FF + f2 * 256, 256, 256)]))
            if l + 1 < DEPTH:
                T.append((8, 512, [("w_ada", l + 1, f2 * 512, 512, 0)]))
        for j in range(8):
            T.append((NFC, 128, [("w_down", l, j * 128, 128, 0)]))
            if j == 0 and l + 1 < DEPTH:
                T.append((8, 512, [("w_ada", l + 1, 11 * 512, 512, 0)]))
    offs = []
    o = 0
    for (K, w, _) in T:
        offs.append(o)
        o += K * w
    return T, offs, o


def pack_tA(tA):
    import ml_dtypes
    out = np.zeros((128, 4, 8, 512), np.float32)
    for tq in range(4):
        for m in range(2):
            out[:, tq, :, m * 256:(m + 1) * 256] = tA[m][:, tq * 256:(tq + 1) * 256].reshape(8, 128, 256).transpose(1, 0, 2)
    return np.ascontiguousarray(out.reshape(128, 4 * 4096).astype(ml_dtypes.bfloat16))


def pack_weights(W, tA=None):
    T, offs, total = tile_specs()
    out = np.zeros((128, total), np.float32)
    for (K, w, pieces), o in zip(T, offs):
        blk = out[:, o:o + K * w].reshape(128, K, w)
        for (name, l, c0, n, dst) in pieces:
            if name == "tA":
                continue
            src = W[name][l]
            blk[:, :, dst:dst + n] = src[:, c0:c0 + n].reshape(K, 128, n).transpose(1, 0, 2)
    return out

class TL:
    def __init__(self, sem, step, name):
        self.sem, self.step, self.cnt, self.name = sem, step, 0, name


class Buf:
    __slots__ = ("name", "w", "rd")

    def __init__(self, name=""):
        self.name, self.w, self.rd = name, None, []


def _compact(rd):
    best = {}
    for tl, v in rd:
        if best.get(tl, 0) < v:
            best[tl] = v
    return list(best.items())


class Sched:
    def __init__(self, nc, stack):
        self.nc = nc
        self.stack = stack
        self.eng = {"pe": nc.tensor, "dve": nc.vector, "act": nc.scalar, "pool": nc.gpsimd, "sp": nc.sync}
        self.tl = {}
        for k in self.eng:
            self.tl[k] = TL(stack.enter_context(nc.semaphore("s_" + k)), 1, k)
        self.seen = {k: {} for k in self.eng}
        self.dma_tls = []

    def dma_tl(self, name):
        t = TL(self.stack.enter_context(self.nc.semaphore("d_" + name)), 16, name)
        self.dma_tls.append(t)
        return t

    def _wait(self, e, deps):
        seen = self.seen[e]
        for tl, v in deps:
            if v <= 0 or seen.get(tl, 0) >= v:
                continue
            if tl.step == 16:
                v = tl.cnt
            self.eng[e].wait_ge(tl.sem, v)
            seen[tl] = v

    def _deps(self, e, reads, writes, is_dma=False):
        deps = {}
        mytl = self.tl[e]

        def add(tl, v):
            if deps.get(tl, 0) < v:
                deps[tl] = v
        for b in reads:
            if b.w is not None:
                tl, v = b.w
                if tl is mytl and e == "pe" and not is_dma:
                    continue
                add(tl, v)
        for b in writes:
            if b.w is not None:
                tl, v = b.w
                if tl is not mytl or is_dma or e != "pe":
                    add(tl, v)
            for tl, v in b.rd:
                if tl is not mytl or is_dma or e != "pe":
                    add(tl, v)
        return list(deps.items())

    def op(self, e, fn, reads=(), writes=(), mark=True):
        self._wait(e, self._deps(e, reads, writes))
        ins = fn()
        tl = self.tl[e]
        if mark:
            tl.cnt += 1
            ins.then_inc(tl.sem, 1)
            v = tl.cnt
        else:
            v = tl.cnt + 1
        for b in reads:
            b.rd.append((tl, v))
            if len(b.rd) > 32:
                b.rd = _compact(b.rd)
        for b in writes:
            b.w = (tl, v)
            b.rd = []
        return ins

    def dma(self, e, out, in_, dtl, reads=(), writes=(), **kw):
        self._wait(e, self._deps(e, reads, writes, is_dma=True))
        ins = self.eng[e].dma_start(out=out, in_=in_, **kw)
        dtl.cnt += 16
        ins.then_inc(dtl.sem, 16)
        v = dtl.cnt
        for b in reads:
            b.rd.append((dtl, v))
        for b in writes:
            b.w = (dtl, v)
            b.rd = []
        return ins

    def barrier(self, engines=("dve", "act", "sp")):
        for e in engines:
            deps = [(self.tl[k], self.tl[k].cnt) for k in self.eng if k != e]
            deps += [(t, t.cnt) for t in self.dma_tls]
            self._wait(e, deps)

    def final_wait(self, e="sp"):
        deps = [(self.tl[k], self.tl[k].cnt) for k in self.eng if k != e]
        deps += [(t, t.cnt) for t in self.dma_tls]
        self._wait(e, deps)


def build_program():
    nc = bass.Bass("TRN2", target_bir_lowering=False)

    def din(name, shape):
        return nc.dram_tensor(name, list(shape), F32, kind="ExternalInput").ap()

    def dout(name, shape):
        return nc.dram_tensor(name, list(shape), F32, kind="ExternalOutput").ap()

    xin = din("xin", [NTOK, D])
    pos = din("pos", [1024, D])
    condT = din("condT", [D, 2])
    C0 = din("C0", [DEPTH, 2, H, DH, DH])
    n0T = din("n0T", [DEPTH, 128, 2, 8])
    m0 = din("m0", [DEPTH, 1, 8])
    keep_d = din("keep", [1, 96])
    TSPEC, TOFF, TTOTAL = tile_specs()
    wpack = din("wpack", [128, TTOTAL])
    tApack = nc.dram_tensor("tApack", [128, 4 * 4096], BF16, kind="ExternalInput").ap()
    badaT = din("badaT", [DEPTH, 128, 96])
    colp_d = din("colp", [DEPTH, 128, NCOL])
    bg_bc = din("bg_bc", [DEPTH, 128, 192])
    bkv_bc = din("bkv_bc", [DEPTH, H, 128, 1024])
    cmat_d = din("cmat", [128, 512])
    dcs_d = din("dcs", [256, 512])
    tB_d = din("tB", [2, 256, 256])
    cmask_d = din("cmask", [128, 2 * NTOK])
    y_d = dout("y", [NTOK, D])
    oC_d = dout("oC", [6, DEPTH, 2, H, DH, DH])
    on_d = dout("on_raw", [DEPTH, 128, 96])
    om_d = dout("om_raw", [DEPTH, 1, 96])

    with ExitStack() as st:
        S = Sched(nc, st)
        V, A, PE = nc.vector, nc.scalar, nc.tensor

        uid = {"n": 0}

        def sb(name, shape, dt=F32, stack=None):
            uid["n"] += 1
            return (stack or st).enter_context(nc.sbuf_tensor("sb%d_%s" % (uid["n"], name), list(shape), dt))

        xT = sb("xT", [128, 8, NTOK]); b_x = [Buf("x%d" % i) for i in range(3)]
        uT = sb("uT", [128, 8, NTOK], BF16); b_u = [Buf("u%d" % i) for i in range(3)]
        NS = 3
        ring = [sb("ring%d" % i, [128, 4096], BF16) for i in range(NS)]
        ring_b = [Buf("ring%d" % i) for i in range(NS)]
        ring_tl = [S.dma_tl("ring%d" % i) for i in range(NS)]
        cmat = sb("cmat", [128, 512]); b_cmat = Buf("cmat")
        cmb = sb("cmb", [128, 384], BF16); b_cmb = Buf("cmb")
        onesb = sb("onesb", [128, 2], BF16)
        cst = sb("cst", [128, 4])
        onesr = sb("onesr", [128, 128], F32R)
        colp = sb("colp", [128, DEPTH, NCOL]); b_colp = Buf("colp")
        kb16 = sb("kb16", [128, DEPTH, 8]); b_kb16 = Buf("kb16")
        modT_all = sb("modT", [128, DEPTH, 96]); b_mods = [Buf("mod%d" % i) for i in range(DEPTH)]
        g1a_all = sb("g1a", [128, DEPTH, 32]); b_g1as = [Buf("g1a%d" % i) for i in range(DEPTH)]
        bada_all = sb("bada", [128, DEPTH, 96]); b_bada = Buf("bada")
        tB = sb("tB", [128, 2, 2, 256], BF16); b_tB = Buf("tB")
        dcs = sb("dcs", [128, 2, 512], BF16); b_dcs = Buf("dcs")
        rows = sb("rows", [1, 1024]); b_rows = Buf("rows"); b_rowsd = [Buf("rows_f"), Buf("rows_b")]
        ones_row = sb("ones_row", [1, 128])
        tabs = sb("tabs", [128, 8, 96]); b_tabs = [Buf("tab%d" % i) for i in range(8)]
        gates = sb("gates", [128, 192]); b_gates = Buf("gates")
        nall = sb("nall", [128, 2, 48]); b_nall = Buf("nall")
        scT = sb("scT", [128, 8, 2], BF16); b_scT = Buf("scT")
        cmat_tl, colp_tl, bada_tl, ct_tl, bgt_tl, rows_tl = [S.dma_tl(n) for n in ("cmat", "colp", "bada", "ct", "bgt", "rows")]
        tB_tl, dcs_tl, cmask_tl = [S.dma_tl(n) for n in ("tB", "dcs", "cmask")]
        out_tl = S.dma_tl("out")
        oc_tls = [S.dma_tl("oc0"), S.dma_tl("oc1")]
        st_tls = [S.dma_tl("cst0"), S.dma_tl("cst1")]
        ys_tls = [S.dma_tl("ys0"), S.dma_tl("ys1")]
        bkv_tl = S.dma_tl("bkv")

        pbank = [st.enter_context(nc.psum_tensor("pb%d" % i, [128, 512], F32)) for i in range(7)]
        pb_b = [Buf("pb%d" % i) for i in range(7)]
        ptb = st.enter_context(nc.psum_tensor("ptb", [128, 1024], BF16))
        ptb_b = [Buf("ptb%d" % i) for i in range(4)]
        pstate = {"i": 0, "t": 0}

        def PB():
            i = pstate["i"]
            pstate["i"] = (i + 1) % 6
            return pbank[i], pb_b[i]

        def PT():
            i = pstate["t"]
            pstate["t"] = (i + 1) % 4
            return ptb[:, i * 256:(i + 1) * 256], ptb_b[i]

        TRIF, TRIB, IDN, ONESM = (cmat[:, 0:128], cmat[:, 128:256], cmat[:, 256:384], cmat[:, 384:512])
        TRIFb, TRIBb, IDNb = (cmb[:, 0:128], cmb[:, 128:256], cmb[:, 256:384])

        wq = {"n": 0, "plan": [], "issued": 0}

        def _issue(idx):
            K_, w_, _ = TSPEC[idx]
            n = K_ * w_
            ch = n if n <= 2048 else n // 2
            assert ch <= 2048 and n % ch == 0
            sl_ = idx % NS
            pieces_ = TSPEC[idx][2]
            if pieces_[0][0] == "tA":
                tq_ = pieces_[0][2] // 256
                src_ = tApack[:, tq_ * 4096:(tq_ + 1) * 4096]
            else:
                src_ = wpack[:, TOFF[idx]:TOFF[idx] + n]
            S.dma("pool", ring[sl_][:, 0:n].rearrange("p (a c) -> p a c", c=ch),
                  src_.rearrange("p (a c) -> p a c", c=ch), ring_tl[sl_], writes=[ring_b[sl_]])

        def wnext(keep=0):
            i = wq["n"]
            released = i - keep
            assert i < released + NS
            while wq["issued"] < min(len(wq["plan"]), released + NS):
                _issue(wq["issued"])
                wq["issued"] += 1
            wq["n"] = i + 1
            return ring[i % NS], ring_b[i % NS]

        wq["plan"] = list(range(len(TSPEC)))
        wq["n"] = 12
        wq["issued"] = 12

        def rv(r, K=8, width=512):
            return r[:, 0:K * width].rearrange("p (k n) -> p k n", k=K)

        S.dma("sp", cmat[:], cmat_d, cmat_tl, writes=[b_cmat])
        S.dma("sp", colp[:], colp_d.rearrange("l p c -> p l c"), colp_tl, writes=[b_colp])
        S.dma("sp", bada_all[:], badaT.rearrange("l p c -> p l c"), bada_tl, writes=[b_bada])
        for m_ in range(2):
            S.dma("pool", tB[:, :, m_, :], tB_d[m_].rearrange("(k p) n -> p k n", p=128), tB_tl, writes=[b_tB])
        S.dma("pool", dcs[:], dcs_d.rearrange("(k p) n -> p k n", p=128), dcs_tl, writes=[b_dcs])
        S.op("dve", lambda: V.tensor_copy(out=cmb[:], in_=cmat[:, 0:384]), reads=[b_cmat], writes=[b_cmb])
        b_ones = Buf("ones")
        S.op("dve", lambda: V.memset(onesb[:], 1.0), writes=[b_ones])
        S.op("dve", lambda: V.memset(ones_row[:], 1.0), writes=[b_ones])
        S.op("dve", lambda: V.memset(cst[:, 0:1], 1.0), writes=[b_ones])
        S.op("act", lambda: A.activation(out=onesr[:], in_=cmat[:, 384:512], func=AF.Copy), reads=[b_cmat], writes=[b_ones])
        S.op("dve", lambda: V.memset(cst[:, 1:2], LN_EPS), writes=[b_ones])
        S.op("dve", lambda: V.memset(cst[:, 2:3], LN_EPS / (ALPHA * ALPHA)), writes=[b_ones])
        S.op("dve", lambda: V.tensor_scalar(out=kb16[:], in0=colp[:, :, 8:16], scalar1=1.0 / 16, scalar2=None,
                                            op0=ALU.mult), reads=[b_colp], writes=[b_kb16])
        with ExitStack() as ph:
            ct = sb("ct", [128, 8, 2], stack=ph); b_ct = Buf("ct")
            S.dma("sp", ct[:], condT.rearrange("(k p) j -> p k j", p=128), ct_tl, writes=[b_ct])
            S.op("act", lambda: A.activation(out=scT[:], in_=ct[:], func=AF.Silu), reads=[b_ct], writes=[b_scT])
            xs = [sb("xs%d" % i, [128, D], stack=ph) for i in range(2)]; b_xs = [Buf("xs0"), Buf("xs1")]
            ps_ = [sb("ps%d" % i, [128, D], stack=ph) for i in range(2)]; b_ps = [Buf("ps0"), Buf("ps1")]
            xs_tl = [S.dma_tl("xs0"), S.dma_tl("xs1")]
            for c in range(NCH):
                i = c % 2
                S.dma("sp", xs[i][:], xin[c * 128:(c + 1) * 128, :], xs_tl[i], writes=[b_xs[i]])
                if c < 8:
                    S.dma("sp", ps_[i][:], pos[c * 128:(c + 1) * 128, :], xs_tl[i], writes=[b_ps[i]])
                    S.op("dve", lambda: V.tensor_tensor(out=xs[i][:], in0=xs[i][:], in1=ps_[i][:], op=ALU.add),
                         reads=[b_xs[i], b_ps[i]], writes=[b_xs[i]])
                for k2 in range(2):
                    pb, bpb = PB()
                    for kk in range(4):
                        k = k2 * 4 + kk
                        S.op("pe", lambda: PE.transpose(out=pb[:, kk * 128:(kk + 1) * 128],
                                                        in_=xs[i][:, k * 128:(k + 1) * 128], identity=IDN),
                             reads=[b_xs[i], b_cmat], writes=[bpb], mark=(kk == 3))
                    e = "act" if k2 == 0 else "dve"
                    dst = xT[:, k2 * 4:k2 * 4 + 4, c * 128:(c + 1) * 128]
                    src = pb[:, :].rearrange("p (k n) -> p k n", k=4)
                    if e == "act":
                        S.op("act", lambda: A.activation(out=dst, in_=src, func=AF.Copy), reads=[bpb], writes=[b_x[c // 4]])
                    else:
                        S.op("dve", lambda: V.tensor_copy(out=dst, in_=src), reads=[bpb], writes=[b_x[c // 4]])
            S.barrier()

        blk_of_unit = {0: [0, 1], 1: [2]}

        def bcol(l, name, j):
            c = BLK[name] * 8 + j
            return colp[:, l, c:c + 1]

        def proj_fm(l, lhs_fn, rhs_t, rhs_b, blk, K=8):
            pb, bpb = PB()
            for k in range(K):
                S.op("pe", lambda: PE.matmul(pb[:, :], lhsT=lhs_fn(k), rhs=rhs_t[:, k, blk * 512:(blk + 1) * 512],
                                             start=(k == 0), stop=(k == K - 1)),
                     reads=rhs_b, writes=[bpb], mark=(k == K - 1))
            return pb, bpb

        for l in range(DEPTH):
            pbm, bpbm = pbank[6], pb_b[6]

            def ada_tile(la, t):
                r, rb = wnext()
                r3 = rv(r)
                for q in range(4):
                    n = t * 4 + q
                    for k in range(8):
                        S.op("pe", lambda: PE.matmul(pbm[:, n * 2:n * 2 + 2], lhsT=r3[:, k, q * 128:(q + 1) * 128],
                                                     rhs=scT[:, k, :], start=(k == 0), stop=(k == 7)),
                             reads=[rb, b_scT], writes=[bpbm], mark=(k == 7))

            def ada_finish(la):
                mT = modT_all[:, la, :]
                S.op("dve", lambda: V.tensor_tensor(out=mT, in0=pbm[:, 0:96], in1=bada_all[:, la, :], op=ALU.add),
                     reads=[bpbm, b_bada], writes=[b_mods[la]])
                for kind in (1, 4):
                    S.op("dve", lambda: V.tensor_scalar(out=mT[:, kind * 16:kind * 16 + 16], in0=mT[:, kind * 16:kind * 16 + 16],
                                                        scalar1=1.0, scalar2=None, op0=ALU.add), reads=[b_mods[la]], writes=[b_mods[la]])
                for gi, kind in enumerate((2, 5)):
                    S.op("dve", lambda: V.tensor_scalar(out=g1a_all[:, la, gi * 16:gi * 16 + 16], in0=mT[:, kind * 16:kind * 16 + 16],
                                                        scalar1=1.0 / ALPHA, scalar2=None, op0=ALU.mult), reads=[b_mods[la]], writes=[b_g1as[la]])
            if l == 0:
                with ExitStack() as ph:
                    NA = 3
                    stg = [sb("astg%d" % i, [128, 4096], stack=ph) for i in range(NA)]; b_stg = [Buf("astg%d" % i) for i in range(NA)]
                    stg_tl = [S.dma_tl("astg%d" % i) for i in range(NA)]
                    wbf = [sb("awb%d" % i, [128, 4096], BF16, stack=ph) for i in range(2)]; b_wbf = [Buf("awb0"), Buf("awb1")]

                    def a_issue(t):
                        S.dma("sp", stg[t % NA][:], wpack[:, TOFF[t]:TOFF[t] + 4096], stg_tl[t % NA], writes=[b_stg[t % NA]])
                    for t in range(NA):
                        a_issue(t)
                    for t in range(12):
                        si, wi_ = t % NA, t % 2
                        S.op("dve", lambda: V.tensor_copy(out=wbf[wi_][:, 0:2048], in_=stg[si][:, 0:2048]), reads=[b_stg[si]], writes=[b_wbf[wi_]])
                        S.op("act", lambda: A.activation(out=wbf[wi_][:, 2048:4096], in_=stg[si][:, 2048:4096], func=AF.Copy), reads=[b_stg[si]], writes=[b_wbf[wi_]])
                        if t + NA < 12:
                            a_issue(t + NA)
                        r3 = rv(wbf[wi_])
                        for q in range(4):
                            n = t * 4 + q
                            for k in range(8):
                                S.op("pe", lambda: PE.matmul(pbm[:, n * 2:n * 2 + 2], lhsT=r3[:, k, q * 128:(q + 1) * 128],
                                                             rhs=scT[:, k, :], start=(k == 0), stop=(k == 7)),
                                     reads=[b_wbf[wi_], b_scT], writes=[bpbm], mark=(k == 7))
                    ada_finish(0)
                    S.barrier()
            modT = modT_all[:, l, :]
            b_mod = b_mods[l]
            b_g1a = b_g1as[l]
            g1a = g1a_all[:, l, :].rearrange("p (g k j) -> p g k j", g=2, k=8)

            def mcol(kind, k, unit):
                c = (kind * 8 + k) * 2 + unit
                return modT[:, c:c + 1]

            def modulate(sh_kind, sc_kind):
                for k in range(8):
                    for unit, (t0, t1) in enumerate(((0, 1024), (1024, 1536))):
                        bl = [b_x[0], b_x[1]] if unit == 0 else [b_x[2]]
                        bu = [b_u[0], b_u[1]] if unit == 0 else [b_u[2]]
                        S.op("dve", lambda: V.tensor_scalar(out=uT[:, k, t0:t1], in0=xT[:, k, t0:t1],
                                                            scalar1=mcol(sc_kind, k, unit), scalar2=mcol(sh_kind, k, unit),
                                                            op0=ALU.mult, op1=ALU.add), reads=bl + [b_mod], writes=bu)
            modulate(0, 1)

            def residual_ln(which, proj_w_fn, rhs_t, rhs_bf, K):
                for j in range(8):
                    lhs_fn = proj_w_fn(j)
                    for blk in range(3):
                        unit = 0 if blk < 2 else 1
                        pb, bpb = proj_fm(l, lhs_fn[0], rhs_t, [rhs_bf[blk], lhs_fn[1]], blk, K=K)
                        S.op("dve", lambda: V.scalar_tensor_tensor(out=xT[:, j, blk * 512:(blk + 1) * 512], in0=pb[:, :],
                                                                   scalar=g1a[:, which, j, unit:unit + 1],
                                                                   in1=xT[:, j, blk * 512:(blk + 1) * 512],
                                                                   op0=ALU.mult, op1=ALU.add),
                             reads=[bpb, b_g1a, b_x[blk]], writes=[b_x[blk]])
                with ExitStack() as ph:
                    sq = [sb("sq%d" % i, [128, 512], stack=ph) for i in range(2)]; b_sq = [Buf("sq0"), Buf("sq1")]
                    rstd = [sb("rstd%d" % i, [128, 512], stack=ph) for i in range(3)]; b_rstd = [Buf("rstd%d" % i) for i in range(3)]
                    cg = C_LNG0 if which == 0 else C_LNG1
                    cb_ = C_LNB0 if which == 0 else C_LNB1
                    sls = [slice(blk * 512, (blk + 1) * 512) for blk in range(3)]
                    pms = []
                    for blk in range(3):
                        pm, bpm = PB()
                        pms.append((pm, bpm))
                        for k in range(8):
                            S.op("pe", lambda: PE.matmul(pm[:, :], lhsT=ONESM, rhs=xT[:, k, sls[blk]], start=(k == 0), stop=(k == 7)),
                                 reads=[b_x[blk], b_cmat], writes=[bpm], mark=(k == 7))
                    bxk = [[Buf("lnx%d_%d" % (blk, k)) for k in range(8)] for blk in range(3)]
                    for blk in range(3):
                        pm, bpm = pms[blk]
                        for k in range(8):
                            S.op("dve", lambda: V.tensor_tensor(out=xT[:, k, sls[blk]], in0=xT[:, k, sls[blk]], in1=pm[:, :], op=ALU.subtract),
                                 reads=[bpm, b_x[blk]], writes=([b_x[blk], bxk[blk][k]] if k == 0 else [bxk[blk][k]]))
                    pvs = []
                    n_ = 0
                    for blk in range(3):
                        pv, bpv = PB()
                        pvs.append((pv, bpv))
                        for k in range(8):
                            i = n_ % 2
                            n_ += 1
                            S.op("act", lambda: A.activation(out=sq[i][:].bitcast(F32R), in_=xT[:, k, sls[blk]], func=AF.Square),
                                 reads=[bxk[blk][k]], writes=[b_sq[i]])
                            S.op("pe", lambda: PE.matmul(pv[:, :], lhsT=onesr[:], rhs=sq[i][:].bitcast(F32R), start=(k == 0), stop=(k == 7)),
                                 reads=[b_sq[i], b_ones], writes=[bpv], mark=True)
                    for blk in range(3):
                        pv, bpv = pvs[blk]
                        S.op("act", lambda: A.activation(out=rstd[blk][:], in_=pv[:, :], func=AF.Sqrt, bias=cst[:, 2:3]), reads=[bpv, b_ones], writes=[b_rstd[blk]])
                        S.op("dve", lambda: V.reciprocal(out=rstd[blk][:], in_=rstd[blk][:]), reads=[b_rstd[blk]], writes=[b_rstd[blk]])
                    for blk in range(3):
                        for k in range(8):
                            S.op("dve", lambda: V.tensor_tensor(out=xT[:, k, sls[blk]], in0=xT[:, k, sls[blk]], in1=rstd[blk][:], op=ALU.mult),
                                 reads=[b_rstd[blk], bxk[blk][k]], writes=[bxk[blk][k]])
                            S.op("act", lambda: A.activation(out=xT[:, k, sls[blk]], in_=xT[:, k, sls[blk]], func=AF.Identity,
                                                             scale=colp[:, l, cg + k:cg + k + 1], bias=colp[:, l, cb_ + k:cb_ + k + 1]),
                                 reads=[bxk[blk][k], b_colp], writes=[bxk[blk][k]])
                    for blk in range(3):
                        S.op("act", lambda: A.activation(out=xT[:, 7, sls[blk]][:, 0:1], in_=xT[:, 7, sls[blk]][:, 0:1], func=AF.Copy),
                             reads=bxk[blk], writes=[b_x[blk]] + bxk[blk])
                    S.barrier()

            with ExitStack() as mix:
                brT = sb("brT", [128, 8, NTOK], BF16, stack=mix); b_br = [Buf("br%d" % i) for i in range(3)]

                r, rb = wnext()
                r3 = rv(r, 8, 16)
                pg, bpg = PB()
                for c in range(NCH):
                    for k in range(8):
                        S.op("pe", lambda: PE.matmul(pg[:, c * 16:(c + 1) * 16], lhsT=uT[:, k, c * 128:(c + 1) * 128], rhs=r3[:, k, 0:16],
                                                     start=(k == 0), stop=(k == 7)), reads=[rb, b_u[c // 4]], writes=[bpg], mark=(k == 7))
                T_L, T_B, T_D, T_W, T_W16, T_FL, T_A, T_X = range(8)
                with ExitStack() as ph:
                    bgt = sb("bgt", [128, 192], stack=ph); b_bgt = Buf("bgt")
                    S.dma("sp", bgt[:], bg_bc[l], bgt_tl, writes=[b_bgt])
                    S.op("dve", lambda: V.tensor_tensor(out=gates[:], in0=pg[:, 0:192], in1=bgt[:], op=ALU.add),
                         reads=[bpg, b_bgt], writes=[b_gates])
                    S.barrier()
                g3 = gates[:, :].rearrange("p (c n) -> p c n", n=16)

                def tab(i):
                    return tabs[:, i, :]

                def tab3(i):
                    return tabs[:, i, :].rearrange("p (c n) -> p c n", n=8)
                for d in range(2):
                    S.op("act", lambda: A.activation(out=tab3(T_L)[:, :, d * 4:d * 4 + 4], in_=g3[:, :, 4 + d * 8:8 + d * 8], func=AF.Exp, scale=-1.0),
                         reads=[b_gates], writes=[b_tabs[T_L]])
                S.op("act", lambda: A.activation(out=tab(T_L), in_=tab(T_L), func=AF.Ln, bias=cst[:, 0:1]), reads=[b_tabs[T_L], b_ones], writes=[b_tabs[T_L]])
                pc, bpc = PB()
                for d in range(2):
                    S.op("pe", lambda: PE.matmul(pc[:, d * 48:(d + 1) * 48].rearrange("p (c n) -> p c n", n=4), lhsT=(TRIF if d == 0 else TRIB),
                                                 rhs=tab3(T_L)[:, :, d * 4:d * 4 + 4], start=True, stop=True),
                         reads=[b_tabs[T_L], b_cmat], writes=[bpc], mark=True)
                for d in range(2):
                    S.op("dve", lambda: V.tensor_scalar(out=tab3(T_B)[:, :, d * 4:d * 4 + 4], in0=pc[:, d * 48:(d + 1) * 48].rearrange("p (c n) -> p c n", n=4),
                                                        scalar1=-1.0, scalar2=None, op0=ALU.mult), reads=[bpc], writes=[b_tabs[T_B]])
                for d in range(2):
                    S.op("dve", lambda: V.tensor_tensor(out=tab3(T_D)[:, :, d * 4:d * 4 + 4], in0=g3[:, :, d * 8:d * 8 + 4],
                                                        in1=tab3(T_B)[:, :, d * 4:d * 4 + 4], op=ALU.subtract),
                         reads=[b_gates, b_tabs[T_B]], writes=[b_tabs[T_D]])
                R_G, R_CM, R_KEEP, R_MP, R_MM, R_MN, R_A, R_T = [i * 96 for i in range(8)]
                pr, bpr = PB()
                S.op("pe", lambda: PE.matmul(pr[0:1, 0:96], lhsT=ONESM[:, 0:1], rhs=tab(T_L), start=True, stop=True),
                     reads=[b_tabs[T_L], b_cmat], writes=[bpr], mark=True)
                S.op("dve", lambda: V.tensor_scalar(out=rows[:, R_G:R_G + 96], in0=pr[0:1, 0:96], scalar1=-1024.0, scalar2=None, op0=ALU.mult),
                     reads=[bpr], writes=[b_rows])
                ptp, bptp = PB()
                S.op("pe", lambda: PE.transpose(out=ptp[0:96, 0:128], in_=tab(T_D), identity=IDN), reads=[b_tabs[T_D], b_cmat], writes=[bptp])
                with ExitStack() as ph:
                    cmc = sb("cmc", [96, 1], stack=ph); b_cmc = Buf("cmc")
                    S.op("dve", lambda: V.reduce_max(out=cmc[:], in_=ptp[0:96, 0:128], axis=AX.X), reads=[bptp], writes=[b_cmc])
                    pr2, bpr2 = PB()
                    S.op("pe", lambda: PE.matmul(pr2[0:1, 0:96], lhsT=cmc[:], rhs=IDN[0:96, 0:96], start=True, stop=True),
                         reads=[b_cmc, b_cmat], writes=[bpr2])
                    S.op("dve", lambda: V.tensor_copy(out=rows[:, R_CM:R_CM + 96], in_=pr2[0:1, 0:96]), reads=[bpr2], writes=[b_rows])
                    S.dma("sp", rows[:, R_KEEP:R_KEEP + 96], keep_d, rows_tl, writes=[b_rows])
                    S.dma("sp", rows[:, R_T:R_T + 8], m0[l], rows_tl, writes=[b_rows])
                    S.barrier()

                def rsl(base, c, d):
                    o = base + c * 8 + d * 4
                    return rows[:, o:o + 4]
                orders = [list(range(12)), list(range(11, -1, -1))]
                starts = [0, 7]
                prevs = [None, None]
                for k_ in range(12):
                    cs = [orders[0][k_], orders[1][k_]]
                    for d in range(2):
                        c = cs[d]
                        carry = rows[:, R_T + d * 4:R_T + d * 4 + 4] if c == starts[d] else (rsl(R_MN, prevs[d], d) if prevs[d] is not None else None)
                        if carry is None:
                            S.op("dve", lambda: V.memset(rsl(R_MP, c, d), 0.0), writes=[b_rowsd[d]])
                        else:
                            S.op("dve", lambda: V.tensor_tensor(out=rsl(R_MP, c, d), in0=carry, in1=rsl(R_KEEP, c, d), op=ALU.mult),
                                 reads=[b_rows, b_rowsd[d]], writes=[b_rowsd[d]])
                    for d in range(2):
                        c = cs[d]
                        S.op("dve", lambda: V.tensor_tensor(out=rsl(R_MM, c, d), in0=rsl(R_MP, c, d), in1=rsl(R_CM, c, d), op=ALU.max),
                             reads=[b_rows, b_rowsd[d]], writes=[b_rowsd[d]])
                    for d in range(2):
                        c = cs[d]
                        S.op("dve", lambda: V.tensor_tensor(out=rsl(R_MN, c, d), in0=rsl(R_MM, c, d), in1=rsl(R_G, c, d), op=ALU.add),
                             reads=[b_rows, b_rowsd[d]], writes=[b_rowsd[d]])
                        prevs[d] = c
                S.op("dve", lambda: V.tensor_copy(out=rows[:, R_A:R_A + 1], in_=rows[:, R_A:R_A + 1]), reads=[b_rowsd[0], b_rowsd[1], b_rows], writes=[b_rows])
                S.op("dve", lambda: V.tensor_tensor(out=rows[:, R_A:R_A + 96], in0=rows[:, R_MP:R_MP + 96], in1=rows[:, R_MM:R_MM + 96], op=ALU.subtract),
                     reads=[b_rows], writes=[b_rows])
                S.op("act", lambda: A.activation(out=rows[:, R_A:R_A + 96], in_=rows[:, R_A:R_A + 96], func=AF.Exp), reads=[b_rows], writes=[b_rows])
                S.op("dve", lambda: V.tensor_tensor(out=rows[:, R_A:R_A + 96], in0=rows[:, R_A:R_A + 96], in1=rows[:, R_KEEP:R_KEEP + 96], op=ALU.mult),
                     reads=[b_rows], writes=[b_rows])
                S.dma("sp", om_d[l], rows[:, R_MN:R_MN + 96], out_tl, reads=[b_rows])
                pbc, bpbc = PB()
                S.op("pe", lambda: PE.matmul(pbc[:, 0:96], lhsT=ones_row[:, :], rhs=rows[:, R_MM:R_MM + 96], start=True, stop=True),
                     reads=[b_rows, b_ones], writes=[bpbc], mark=False)
                S.op("pe", lambda: PE.matmul(pbc[:, 96:192], lhsT=ones_row[:, :], rhs=rows[:, R_A:R_A + 96], start=True, stop=True),
                     reads=[b_rows, b_ones], writes=[bpbc], mark=True)
                S.op("dve", lambda: V.tensor_copy(out=tab(T_A), in_=pbc[:, 96:192]), reads=[bpbc], writes=[b_tabs[T_A]])
                S.op("dve", lambda: V.tensor_tensor(out=tab(T_W), in0=tab(T_D), in1=pbc[:, 0:96], op=ALU.subtract),
                     reads=[bpbc, b_tabs[T_D]], writes=[b_tabs[T_W]])
                S.op("act", lambda: A.activation(out=tab(T_W), in_=tab(T_W), func=AF.Exp), reads=[b_tabs[T_W]], writes=[b_tabs[T_W]])
                S.op("dve", lambda: V.tensor_scalar(out=tab(T_W16), in0=tab(T_W), scalar1=1.0 / 16, scalar2=None, op0=ALU.mult),
                     reads=[b_tabs[T_W]], writes=[b_tabs[T_W16]])
                S.op("dve", lambda: V.tensor_tensor(out=tab(T_FL), in0=tab(T_B), in1=pbc[:, 0:96], op=ALU.add),
                     reads=[bpbc, b_tabs[T_B]], writes=[b_tabs[T_FL]])
                S.op("act", lambda: A.activation(out=tab(T_FL), in_=tab(T_FL), func=AF.Exp, scale=-1.0), reads=[b_tabs[T_FL]], writes=[b_tabs[T_FL]])

                with ExitStack() as ml:
                    qT = sb("qT", [128, 2, NTOK], BF16, stack=ml); b_q = Buf("q")
                    kT = sb("kT", [128, 2, NTOK], BF16, stack=ml); b_k = Buf("k")
                    ktok = sb("ktok", [128, NCH, 256], BF16, stack=ml); b_kt = Buf("kt")
                    vaug = sb("vaug", [128, NCH, 258], BF16, stack=ml); b_v = Buf("v")
                    hsum = sb("hsum", [128, NCH, 256], stack=ml); b_hs = [Buf("hs%d" % c) for c in range(NCH)]
                    bkv = sb("bkv", [128, 1024], stack=ml); b_bkv = Buf("bkv")
                    bkv2 = bkv[:, :].rearrange("p (t a n) -> p t a n", t=2, a=2)
                    NCHAIN = 4
                    Cst = [sb("Cst%d" % d, [128, 2, 257], stack=ml) for d in range(NCHAIN)]; b_Cst = [Buf("Cst%d" % d) for d in range(NCHAIN)]
                    Cb = [sb("Cb%d" % i, [128, 2, 258], BF16, stack=ml) for i in range(NCHAIN)]; b_Cb = [Buf("Cb%d" % i) for i in range(NCHAIN)]
                    Cstage = [sb("Cstage%d" % i, [128, 2, 256], stack=ml) for i in range(2)]; b_Cstage = [Buf("Cstage0"), Buf("Cstage1")]
                    STt = [sb("ST%d" % i, [128, 128], BF16, stack=ml) for i in range(NCHAIN)]; b_ST = [Buf("ST%d" % i) for i in range(NCHAIN)]
                    kw = [sb("kw%d" % i, [128, 256], BF16, stack=ml) for i in range(NCHAIN)]; b_kw = [Buf("kw%d" % i) for i in range(NCHAIN)]
                    hn = [sb("hn%d" % i, [128, 256], BF16, stack=ml) for i in range(2)]; b_hn = [Buf("hn0"), Buf("hn1")]
                    sgo = [sb("sgo%d" % i, [128, 512], BF16, stack=ml) for i in range(2)]; b_sgo = [Buf("sgo0"), Buf("sgo1")]
                    tiny = sb("tiny", [128, 8], stack=ml); b_tiny = [Buf("tiny%d" % i) for i in range(8)]
                    stats = sb("stats", [128, NCH, 6], stack=ml); b_stats = Buf("stats")
                    mv = sb("mv", [128, NCH, 2], stack=ml); b_mv = Buf("mv")
                    S.op("dve", lambda: V.memset(vaug[:, :, 256:258], 1.0), writes=[b_v])
                    for d_ in range(NCHAIN):
                        S.op("dve", lambda: V.memset(Cst[d_][:], 0.0), writes=[b_Cst[d_]])
                    cnt = {"c": 0, "stg": 0}
                    chains = [dict(d=0, chunks=list(range(0, 8)), load=True), dict(d=1, chunks=list(range(7, -1, -1)), load=True),
                              dict(d=0, chunks=list(range(8, 12)), load=False), dict(d=1, chunks=list(range(11, 7, -1)), load=False)]
                    for h in range(H):
                        S.dma("sp", bkv[:], bkv_bc[l, h], bkv_tl, writes=[b_bkv])
                        r, rb = wnext(); r3 = rv(r)
                        for blk in range(3):
                            for ec in range(2):
                                pb, bpb = proj_fm(l, lambda k: r3[:, k, ec * 128:(ec + 1) * 128], uT, [rb, b_u[blk]], blk)
                                S.op("act", lambda: A.activation(out=qT[:, ec, blk * 512:(blk + 1) * 512], in_=pb[:, :], func=AF.Identity,
                                                                 bias=bcol(l, "q", h * 2 + ec)), reads=[bpb, b_colp], writes=[b_q])
                                pb, bpb = proj_fm(l, lambda k: r3[:, k, 256 + ec * 128:256 + (ec + 1) * 128], uT, [rb, b_u[blk]], blk)
                                S.op("act", lambda: A.activation(out=kT[:, ec, blk * 512:(blk + 1) * 512], in_=pb[:, :], func=AF.Identity,
                                                                 bias=kb16[:, l, h * 2 + ec:h * 2 + ec + 1], scale=1.0 / 16),
                                     reads=[bpb, b_kb16], writes=[b_k])
                        for c2 in range(NCH // 2):
                            pb, bpb = PB()
                            for cc in range(2):
                                c = c2 * 2 + cc
                                for k in range(8):
                                    S.op("pe", lambda: PE.matmul(pb[:, cc * 256:(cc + 1) * 256], lhsT=uT[:, k, c * 128:(c + 1) * 128], rhs=r3[:, k, 256:512],
                                                                 start=(k == 0), stop=(k == 7)), reads=[rb, b_u[c // 4]], writes=[bpb], mark=(k == 7))
                            S.op("dve", lambda: V.tensor_tensor(out=ktok[:, c2 * 2:c2 * 2 + 2, :], in0=pb[:, :].rearrange("p (a n) -> p a n", a=2),
                                                                in1=bkv2[:, 0, :, :], op=ALU.add),
                                 reads=[bpb, b_bkv], writes=[b_kt])
                        r2, rb2 = wnext(); r23 = rv(r2)
                        for c2 in range(NCH // 2):
                            pb, bpb = PB()
                            for cc in range(2):
                                c = c2 * 2 + cc
                                for k in range(8):
                                    S.op("pe", lambda: PE.matmul(pb[:, cc * 256:(cc + 1) * 256], lhsT=uT[:, k, c * 128:(c + 1) * 128], rhs=r23[:, k, 0:256],
                                                                 start=(k == 0), stop=(k == 7)), reads=[rb2, b_u[c // 4]], writes=[bpb], mark=(k == 7))
                            S.op("dve", lambda: V.tensor_tensor(out=vaug[:, c2 * 2:c2 * 2 + 2, 0:256], in0=pb[:, :].rearrange("p (a n) -> p a n", a=2),
                                                                in1=bkv2[:, 1, :, :], op=ALU.add),
                                 reads=[bpb, b_bkv], writes=[b_v])
                        hs_written = [False] * NCH
                        for ci, ch in enumerate(chains):
                            if ch["load"]:
                                d = ch["d"]
                                S.dma("sp", Cst[ci][:, :, 0:256], C0[l, d, h].rearrange("(k p) e -> p k e", p=128), st_tls[ci], writes=[b_Cst[ci]])
                                S.dma("sp", Cst[ci][:, :, 256:257], n0T[l, :, :, d * 4 + h:d * 4 + h + 1], st_tls[ci], writes=[b_Cst[ci]],
                                      allow_slow_non_contiguous=True)
                        for k_, grp in [(k__, g__) for k__ in range(8) for g__ in ((0, 1), (2, 3))]:
                            act = [(ci, chains[ci], chains[ci]["chunks"][k_]) for ci in grp if k_ < len(chains[ci]["chunks"])]
                            if not act:
                                continue
                            pq, bpq = PB()
                            pns = [PB() for _ in act]
                            for ai, (ci, ch, c) in enumerate(act):
                                csl = slice(c * 128, (c + 1) * 128)
                                for dk in range(2):
                                    S.op("pe", lambda: PE.matmul(pq[:, ai * 128:(ai + 1) * 128], lhsT=kT[:, dk, csl], rhs=qT[:, dk, csl], start=(dk == 0), stop=(dk == 1)),
                                         reads=[b_q, b_k], writes=[bpq], mark=(dk == 1))
                            for ai, (ci, ch, c) in enumerate(act):
                                col = c * 8 + ch["d"] * 4 + h
                                maskT = TRIFb if ch["d"] == 0 else TRIBb
                                S.op("dve", lambda: V.scalar_tensor_tensor(out=STt[ci][:], in0=pq[:, ai * 128:(ai + 1) * 128], scalar=tabs[:, T_W, col:col + 1], in1=maskT,
                                                                           op0=ALU.mult, op1=ALU.mult), reads=[bpq, b_tabs[T_W], b_cmb], writes=[b_ST[ci]])
                                if ci % 2 == 0:
                                    S.op("act", lambda: A.activation(out=kw[ci][:], in_=ktok[:, c, :], func=AF.Copy, scale=tabs[:, T_W16, col:col + 1]),
                                         reads=[b_kt, b_tabs[T_W16]], writes=[b_kw[ci]])
                                else:
                                    S.op("dve", lambda: V.tensor_scalar(out=kw[ci][:], in0=ktok[:, c, :], scalar1=tabs[:, T_W16, col:col + 1], scalar2=None,
                                                                        op0=ALU.mult), reads=[b_kt, b_tabs[T_W16]], writes=[b_kw[ci]])
                            pus = []
                            for ai, (ci, ch, c) in enumerate(act):
                                pu, bpu = PB()
                                for dk in range(2):
                                    S.op("pe", lambda: PE.matmul(pu[:, dk * 256:(dk + 1) * 256], lhsT=kw[ci][:, dk * 128:(dk + 1) * 128], rhs=vaug[:, c, 0:256], start=True, stop=True),
                                         reads=[b_kw[ci], b_v], writes=[bpu], mark=False)
                                    S.op("pe", lambda: PE.matmul(pns[ai][0][:, 384 + dk:385 + dk], lhsT=kw[ci][:, dk * 128:(dk + 1) * 128], rhs=vaug[:, c, 256:257], start=True, stop=True),
                                         reads=[b_kw[ci], b_v], writes=[bpu, pns[ai][1]], mark=(dk == 1))
                                pus.append((pu, bpu))
                            for ci, ch, c in act:
                                col = c * 8 + ch["d"] * 4 + h
                                S.op("act", lambda: A.activation(out=Cb[ci][:, :, 0:257], in_=Cst[ci][:], func=AF.Copy, scale=tabs[:, T_A, col:col + 1]),
                                     reads=[b_Cst[ci], b_tabs[T_A]], writes=[b_Cb[ci]])
                            for ai, (ci, ch, c) in enumerate(act):
                                csl = slice(c * 128, (c + 1) * 128)
                                pn, bpn = pns[ai]
                                for dk in range(2):
                                    S.op("pe", lambda: PE.matmul(pn[:, 0:257], lhsT=qT[:, dk, csl], rhs=Cb[ci][:, dk, 0:257], start=(dk == 0), stop=False),
                                         reads=[b_q, b_Cb[ci]], writes=[bpn], mark=False)
                                S.op("pe", lambda: PE.matmul(pn[:, 0:257], lhsT=STt[ci][:], rhs=vaug[:, c, 0:257], start=False, stop=True),
                                     reads=[b_ST[ci], b_v], writes=[bpn], mark=True)
                            for ai, (ci, ch, c) in enumerate(act):
                                pu, bpu = pus[ai]
                                col = c * 8 + ch["d"] * 4 + h
                                S.op("dve", lambda: V.scalar_tensor_tensor(out=Cst[ci][:, :, 0:256], in0=Cst[ci][:, :, 0:256], scalar=tabs[:, T_A, col:col + 1],
                                                                           in1=pu[:, :].rearrange("p (a n) -> p a n", a=2), op0=ALU.mult, op1=ALU.add),
                                     reads=[bpu, b_Cst[ci], b_tabs[T_A]], writes=[b_Cst[ci]])
                            for ai, (ci, ch, c) in enumerate(act):
                                col = c * 8 + ch["d"] * 4 + h
                                S.op("dve", lambda: V.scalar_tensor_tensor(out=Cst[ci][:, :, 256:257], in0=Cst[ci][:, :, 256:257], scalar=tabs[:, T_A, col:col + 1],
                                                                           in1=pns[ai][0][:, 384:386].rearrange("p (a n) -> p a n", n=1), op0=ALU.mult, op1=ALU.add),
                                     reads=[pns[ai][1], b_Cst[ci], b_tabs[T_A]], writes=[b_Cst[ci]])
                            seqs = []
                            for ai, (ci, ch, c) in enumerate(act):
                                d = ch["d"]
                                col = c * 8 + d * 4 + h
                                pn, bpn = pns[ai]
                                it = cnt["c"] % 8
                                cnt["c"] += 1
                                tn = tiny[:, it:it + 1]

                                def mk(pn=pn, bpn=bpn, it=it, tn=tn, col=col, c=c):
                                    o = []
                                    o.append(lambda: S.op("dve", lambda: V.tensor_tensor(out=tn, in0=pn[:, 256:257], in1=tabs[:, T_FL, col:col + 1], op=ALU.max),
                                                          reads=[bpn, b_tabs[T_FL]], writes=[b_tiny[it]]))
                                    o.append(lambda: S.op("dve", lambda: V.scalar_tensor_tensor(out=tn, in0=pn[:, 256:257], scalar=-1.0, in1=tn, op0=ALU.mult, op1=ALU.max),
                                                          reads=[bpn, b_tiny[it]], writes=[b_tiny[it]]))
                                    o.append(lambda: S.op("dve", lambda: V.reciprocal(out=tn, in_=tn), reads=[b_tiny[it]], writes=[b_tiny[it]]))
                                    if not hs_written[c]:
                                        hs_written[c] = True
                                        o.append(lambda: S.op("dve", lambda: V.tensor_scalar(out=hsum[:, c, :], in0=pn[:, 0:256], scalar1=tn, scalar2=None, op0=ALU.mult),
                                                              reads=[bpn, b_tiny[it]], writes=[b_hs[c]]))
                                    else:
                                        o.append(lambda: S.op("dve", lambda: V.scalar_tensor_tensor(out=hsum[:, c, :], in0=pn[:, 0:256], scalar=tn, in1=hsum[:, c, :],
                                                                                                    op0=ALU.mult, op1=ALU.add), reads=[bpn, b_tiny[it], b_hs[c]], writes=[b_hs[c]]))
                                    return o
                                seqs.append(mk())
                            for step_ in range(4):
                                for o in seqs:
                                    o[step_]()
                            for ai, (ci, ch, c) in enumerate(act):
                                d = ch["d"]
                                seg_end = (c % 2 == 1) if d == 0 else (c % 2 == 0)
                                if seg_end:
                                    seg = c // 2
                                    idx = seg * 8 + d * 4 + h
                                    sg_ = cnt["stg"] % 2
                                    cnt["stg"] += 1
                                    S.op("act", lambda: A.activation(out=Cstage[sg_][:], in_=Cst[ci][:, :, 0:256], func=AF.Copy), reads=[b_Cst[ci]], writes=[b_Cstage[sg_]])
                                    S.op("dve", lambda: V.tensor_copy(out=nall[:, :, idx:idx + 1], in_=Cst[ci][:, :, 256:257]), reads=[b_Cst[ci]], writes=[b_nall])
                                    S.dma("sp", oC_d[seg, l, d, h].rearrange("(k p) e -> p k e", p=128), Cstage[sg_][:], oc_tls[sg_], reads=[b_Cstage[sg_]])
                        for c in range(NCH):
                            S.op("dve", lambda: V.bn_stats(out=stats[:, c, :], in_=hsum[:, c, :]), reads=[b_hs[c]], writes=[b_stats])
                        for c in range(NCH):
                            S.op("dve", lambda: V.bn_aggr(out=mv[:, c, :], in_=stats[:, c, :]), reads=[b_stats], writes=[b_mv])
                        S.op("act", lambda: A.activation(out=mv[:, :, 1:2], in_=mv[:, :, 1:2], func=AF.Sqrt, bias=cst[:, 1:2]), reads=[b_mv, b_ones], writes=[b_mv])
                        S.op("dve", lambda: V.reciprocal(out=mv[:, :, 1:2], in_=mv[:, :, 1:2]), reads=[b_mv], writes=[b_mv])
                        for blk in range(3):
                            for ec in range(2):
                                pb, bpb = proj_fm(l, lambda k: r23[:, k, 256 + ec * 128:256 + (ec + 1) * 128], uT, [rb2, b_u[blk]], blk)
                                S.op("act", lambda: A.activation(out=sgo[ec][:], in_=pb[:, :], func=AF.Sigmoid, bias=bcol(l, "o", h * 2 + ec)),
                                     reads=[bpb, b_colp], writes=[b_sgo[ec]])
                            for cc in range(4):
                                c = blk * 4 + cc
                                i2 = c % 2
                                S.op("dve", lambda: V.tensor_scalar(out=hn[i2][:], in0=hsum[:, c, :], scalar1=mv[:, c, 0:1], scalar2=mv[:, c, 1:2],
                                                                    op0=ALU.subtract, op1=ALU.mult), reads=[b_hs[c], b_mv], writes=[b_hn[i2]])
                                pt, bpt = PT()
                                for ec in range(2):
                                    S.op("pe", lambda: PE.transpose(out=pt[:, ec * 128:(ec + 1) * 128], in_=hn[i2][:, ec * 128:(ec + 1) * 128], identity=IDNb),
                                         reads=[b_hn[i2], b_cmb], writes=[bpt], mark=(ec == 1))
                                for ec in range(2):
                                    S.op("dve", lambda: V.scalar_tensor_tensor(out=brT[:, h * 2 + ec, c * 128:(c + 1) * 128], in0=pt[:, ec * 128:(ec + 1) * 128],
                                                                               scalar=colp[:, l, C_MNG + h * 2 + ec:C_MNG + h * 2 + ec + 1],
                                                                               in1=sgo[ec][:, cc * 128:(cc + 1) * 128], op0=ALU.mult, op1=ALU.mult),
                                         reads=[bpt, b_colp, b_sgo[ec]], writes=[b_br[blk]])
                    S.dma("sp", on_d[l], nall[:, :, :].rearrange("p a b -> p (a b)"), out_tl, reads=[b_nall])
                    S.barrier()

                merged = sb("merged", [128, 8, NTOK], stack=mix); b_mg = [Buf("mg%d" % i) for i in range(3)]

                def branch_merge(gname, first):
                    with ExitStack() as ph:
                        sg = [sb("sg%d" % i, [128, 512], stack=ph) for i in range(2)]; b_sg = [Buf("sg0"), Buf("sg1")]
                        n = 0
                        for jq in range(4):
                            rw, rwb = wnext(); rw3 = rv(rw)
                            rgb = rwb
                            for q in range(2):
                                j = jq * 2 + q
                                for blk in range(3):
                                    sl = slice(blk * 512, (blk + 1) * 512)
                                    i2 = n % 2
                                    n += 1
                                    pg_, bpg_ = proj_fm(l, lambda k: rw3[:, k, 256 + q * 128:256 + (q + 1) * 128], uT, [rgb, b_u[blk]], blk)
                                    S.op("act", lambda: A.activation(out=sg[i2][:], in_=pg_[:, :], func=AF.Sigmoid, bias=bcol(l, gname, j)),
                                         reads=[bpg_, b_colp], writes=[b_sg[i2]])
                                    pp, bpp = proj_fm(l, lambda k: rw3[:, k, q * 128:(q + 1) * 128], brT, [rwb, b_br[blk]], blk)
                                    if first:
                                        S.op("dve", lambda: V.tensor_tensor(out=merged[:, j, sl], in0=pp[:, :], in1=sg[i2][:], op=ALU.mult),
                                             reads=[bpp, b_sg[i2]], writes=[b_mg[blk]])
                                    else:
                                        S.op("dve", lambda: V.tensor_tensor(out=sg[i2][:], in0=pp[:, :], in1=sg[i2][:], op=ALU.mult),
                                             reads=[bpp, b_sg[i2]], writes=[b_sg[i2]])
                                        S.op("dve", lambda: V.tensor_tensor(out=merged[:, j, sl], in0=merged[:, j, sl], in1=sg[i2][:], op=ALU.add),
                                             reads=[b_sg[i2], b_mg[blk]], writes=[b_mg[blk]])
                        S.barrier()
                branch_merge("gm", True)

                with ExitStack() as ft:
                    xf = sb("xf", [128, 2, NTOK], BF16, stack=ft); b_xf = Buf("xf")
                    Y = sb("Y", [128, NCH, 512], BF16, stack=ft); b_Y = Buf("Y")
                    for g in range(4):
                        if True:
                            r, rb = wnext(); r3 = rv(r, 8, 256)
                            gg = 0
                            for blk in range(3):
                                for ec in range(2):
                                    pb, bpb = proj_fm(l, lambda k: r3[:, k, gg * 256 + ec * 128:gg * 256 + (ec + 1) * 128], uT, [rb, b_u[blk]], blk)
                                    S.op("act", lambda: A.activation(out=xf[:, ec, blk * 512:(blk + 1) * 512], in_=pb[:, :], func=AF.Identity,
                                                                     bias=bcol(l, "f", g * 2 + ec)), reads=[bpb, b_colp], writes=[b_xf])
                            for c in range(NCH):
                                pb, bpb = PB()
                                for ec in range(2):
                                    S.op("pe", lambda: PE.matmul(pb[:, :], lhsT=xf[:, ec, c * 128:(c + 1) * 128], rhs=dcs[:, ec, :], start=(ec == 0), stop=(ec == 1)),
                                         reads=[b_xf, b_dcs], writes=[bpb], mark=(ec == 1))
                                if c % 2 == 0:
                                    S.op("act", lambda: A.activation(out=Y[:, c, :], in_=pb[:, :], func=AF.Copy), reads=[bpb], writes=[b_Y])
                                else:
                                    S.op("dve", lambda: V.tensor_copy(out=Y[:, c, :], in_=pb[:, :]), reads=[bpb], writes=[b_Y])
                            for tq in range(4):
                                rc, rcb = wnext(); rc3 = rv(rc)
                                pb, bpb = PB()
                                for ec in range(2):
                                    for tc_ in range(8):
                                        S.op("pe", lambda: PE.matmul(pb[:, ec * 256:(ec + 1) * 256], lhsT=Y[:, tc_, ec * 128:(ec + 1) * 128], rhs=rc3[:, tc_, 0:256], start=(tc_ == 0), stop=False),
                                             reads=[b_Y, rcb], writes=[bpb], mark=False)
                                        S.op("pe", lambda: PE.matmul(pb[:, ec * 256:(ec + 1) * 256], lhsT=Y[:, tc_, 256 + ec * 128:256 + (ec + 1) * 128], rhs=rc3[:, tc_, 256:512], start=False, stop=(tc_ == 7)),
                                             reads=[b_Y, rcb], writes=[bpb], mark=(tc_ == 7))
                                S.op("act", lambda: A.activation(out=brT[:, g * 2:g * 2 + 2, tq * 256:(tq + 1) * 256], in_=pb[:, :].rearrange("p (a n) -> p a n", a=2), func=AF.Copy),
                                     reads=[bpb], writes=[b_br[tq // 2]])
                            for ec in range(2):
                                pb, bpb = PB()
                                for sg_ in range(2):
                                    for tc_ in range(2):
                                        c = 8 + sg_ * 2 + tc_
                                        S.op("pe", lambda: PE.matmul(pb[:, sg_ * 256:(sg_ + 1) * 256], lhsT=Y[:, c, ec * 128:(ec + 1) * 128], rhs=tB[:, tc_, 0, :],
                                                                     start=(tc_ == 0), stop=False), reads=[b_Y, b_tB], writes=[bpb], mark=False)
                                        S.op("pe", lambda: PE.matmul(pb[:, sg_ * 256:(sg_ + 1) * 256], lhsT=Y[:, c, 256 + ec * 128:256 + (ec + 1) * 128], rhs=tB[:, tc_, 1, :],
                                                                     start=False, stop=(tc_ == 1)), reads=[b_Y, b_tB], writes=[bpb], mark=(tc_ == 1))
                                S.op("dve", lambda: V.tensor_copy(out=brT[:, g * 2 + ec, 1024:1536], in_=pb[:, :]), reads=[bpb], writes=[b_br[2]])
                    S.barrier()
                branch_merge("gf", False)

                with ExitStack() as cv:
                    cmask = sb("cmask", [128, 2, NTOK], BF16, stack=cv); b_cmk = Buf("cmask")
                    S.barrier(("pool",))
                    S.dma("pool", cmask[:], cmask_d.rearrange("p (a t) -> p a t", a=2), cmask_tl, writes=[b_cmk])
                    uc = sb("uc", [128, NTOK + 2], stack=cv); b_uc = Buf("uc")
                    ccs = [sb("ccs%d" % i, [128, 512], stack=cv) for i in range(2)]; b_ccs = [Buf("ccs0"), Buf("ccs1")]
                    t1 = [sb("t1%d" % i, [128, 512], stack=cv) for i in range(2)]; b_t1 = [Buf("t10"), Buf("t11")]
                    t2 = [sb("t2%d" % i, [128, 8], stack=cv) for i in range(2)]; b_t2 = [Buf("t20"), Buf("t21")]
                    ncwt = sb("ncwt", [128, 24], stack=cv); b_ncw = Buf("ncw")
                    S.op("dve", lambda: V.tensor_scalar(out=ncwt[:], in0=colp[:, l, C_W0:C_W0 + 24], scalar1=-1.0, scalar2=None, op0=ALU.mult),
                         reads=[b_colp], writes=[b_ncw])
                    S.op("dve", lambda: V.memset(uc[:], 0.0), writes=[b_uc])
                    n = 0
                    for j in range(8):
                        r, rb = wnext(); r3 = rv(r, 8, 384)
                        pcb = []
                        for blk in range(3):
                            sl = slice(blk * 512, (blk + 1) * 512)
                            i2 = n % 2
                            n += 1
                            pc_, bpc_ = proj_fm(l, lambda k: r3[:, k, 0:128], uT, [rb, b_u[blk]], blk)
                            S.op("act", lambda: A.activation(out=ccs[i2][:], in_=pc_[:, :], func=AF.Identity, bias=bcol(l, "cc", j)),
                                 reads=[bpc_, b_colp], writes=[b_ccs[i2]])
                            px_, bpx_ = proj_fm(l, lambda k: r3[:, k, 128:256], uT, [rb, b_u[blk]], blk)
                            S.op("dve", lambda: V.scalar_tensor_tensor(out=uc[:, 1 + blk * 512:1 + (blk + 1) * 512], in0=px_[:, :], scalar=bcol(l, "cx", j), in1=ccs[i2][:],
                                                                       op0=ALU.add, op1=ALU.mult), reads=[bpx_, b_colp, b_ccs[i2]], writes=[b_uc])
                        for blk in range(3):
                            sl = slice(blk * 512, (blk + 1) * 512)
                            i2 = n % 2
                            n += 1
                            pb_, bpb_ = proj_fm(l, lambda k: r3[:, k, 256:384], uT, [rb, b_u[blk]], blk)
                            cw = lambda q: colp[:, l, q + j:q + j + 1]
                            S.op("act", lambda: A.activation(out=t1[i2][:], in_=uc[:, 1 + blk * 512:1 + (blk + 1) * 512], func=AF.Identity,
                                                             scale=cw(C_W1), bias=cw(C_CB)), reads=[b_uc, b_colp], writes=[b_t1[i2]])
                            Lv = uc[:, blk * 512:(blk + 1) * 512]
                            Rv = uc[:, 2 + blk * 512:2 + (blk + 1) * 512]
                            S.op("dve", lambda: V.scalar_tensor_tensor(out=t1[i2][:], in0=Lv, scalar=cw(C_W0), in1=t1[i2][:], op0=ALU.mult, op1=ALU.add),
                                 reads=[b_uc, b_t1[i2], b_colp], writes=[b_t1[i2]])
                            S.op("dve", lambda: V.scalar_tensor_tensor(out=t1[i2][:], in0=Rv, scalar=cw(C_W2), in1=t1[i2][:], op0=ALU.mult, op1=ALU.add),
                                 reads=[b_uc, b_t1[i2], b_colp], writes=[b_t1[i2]])
                            st64 = lambda ap, o: ap.rearrange("p (a b) -> p a b", b=64)[:, :, o]
                            ncw = lambda q: ncwt[:, q - C_W0 + j:q - C_W0 + j + 1]
                            S.op("dve", lambda: V.tensor_tensor(out=t2[0][:, 0:8], in0=st64(Lv, 0), in1=st64(cmask[:, 0, sl], 0), op=ALU.mult),
                                 reads=[b_uc, b_cmk], writes=[b_t2[0]])
                            S.op("dve", lambda: V.scalar_tensor_tensor(out=st64(t1[i2][:], 0), in0=t2[0][:, 0:8], scalar=ncw(C_W0), in1=st64(t1[i2][:], 0),
                                                                       op0=ALU.mult, op1=ALU.add), reads=[b_t2[0], b_t1[i2], b_ncw], writes=[b_t1[i2]])
                            S.op("dve", lambda: V.tensor_tensor(out=t2[1][:, 0:8], in0=st64(Rv, 63), in1=st64(cmask[:, 1, sl], 63), op=ALU.mult),
                                 reads=[b_uc, b_cmk], writes=[b_t2[1]])
                            S.op("dve", lambda: V.scalar_tensor_tensor(out=st64(t1[i2][:], 63), in0=t2[1][:, 0:8], scalar=ncw(C_W2), in1=st64(t1[i2][:], 63),
                                                                       op0=ALU.mult, op1=ALU.add), reads=[b_t2[1], b_t1[i2], b_ncw], writes=[b_t1[i2]])
                            S.op("dve", lambda: V.scalar_tensor_tensor(out=brT[:, j, sl], in0=pb_[:, :], scalar=bcol(l, "cb", j), in1=t1[i2][:],
                                                                       op0=ALU.add, op1=ALU.mult), reads=[bpb_, b_colp, b_t1[i2]], writes=[b_br[blk]])
                    S.barrier()
                branch_merge("gc", False)
                for j in range(8):
                    for blk in range(3):
                        sl = slice(blk * 512, (blk + 1) * 512)
                        S.op("act", lambda: A.activation(out=brT[:, j, sl], in_=merged[:, j, sl], func=AF.Copy), reads=[b_mg[blk]], writes=[b_br[blk]])
                ws = {}

                def wout_fn(j):
                    if j % 4 == 0:
                        ws["r"], ws["b"] = wnext()
                    r3 = rv(ws["r"])
                    q = j % 4
                    return (lambda k: r3[:, k, q * 128:(q + 1) * 128], ws["b"])
                residual_ln(0, wout_fn, brT, b_br, 8)

            modulate(3, 4)
            with ExitStack() as ff:
                gT = sb("gT", [128, NFC, NTOK], BF16, stack=ff); b_g = [Buf("g%d" % i) for i in range(3)]
                sa = [sb("sa%d" % i, [128, 512], stack=ff) for i in range(2)]; b_sa = [Buf("sa0"), Buf("sa1")]
                n = 0
                for f2 in range(11):
                    r, rb = wnext(); r3 = rv(r)
                    for q in range(2):
                        f = f2 * 2 + q
                        for blk in range(3):
                            sl = slice(blk * 512, (blk + 1) * 512)
                            i2 = n % 2
                            n += 1
                            pa, bpa = proj_fm(l, lambda k: r3[:, k, q * 128:(q + 1) * 128], uT, [rb, b_u[blk]], blk)
                            S.op("act", lambda: A.activation(out=sa[i2][:], in_=pa[:, :], func=AF.Silu), reads=[bpa], writes=[b_sa[i2]])
                            pb2, bpb2 = proj_fm(l, lambda k: r3[:, k, 256 + q * 128:256 + (q + 1) * 128], uT, [rb, b_u[blk]], blk)
                            S.op("dve", lambda: V.tensor_tensor(out=gT[:, f, sl], in0=pb2[:, :], in1=sa[i2][:], op=ALU.mult),
                                 reads=[bpb2, b_sa[i2]], writes=[b_g[blk]])
                    if l + 1 < DEPTH:
                        ada_tile(l + 1, f2)

                def wdn_fn(j):
                    if j == 1 and l + 1 < DEPTH:
                        ada_tile(l + 1, 11)
                        ada_finish(l + 1)
                    r, rb = wnext()
                    r3 = rv(r, K=NFC, width=128)
                    return (lambda k: r3[:, k, :], rb)
                residual_ln(1, wdn_fn, gT, b_g, NFC)

        with ExitStack() as ph:
            ys = [sb("ys%d" % i, [128, D], stack=ph) for i in range(2)]; b_ys = [Buf("ys0"), Buf("ys1")]
            for c in range(NCH):
                i = c % 2
                for k2 in range(2):
                    pb, bpb = PB()
                    for kk in range(4):
                        k = k2 * 4 + kk
                        S.op("pe", lambda: PE.transpose(out=pb[:, kk * 128:(kk + 1) * 128], in_=xT[:, k, c * 128:(c + 1) * 128], identity=IDN),
                             reads=[b_x[c // 4], b_cmat], writes=[bpb], mark=(kk == 3))
                    if k2 == 0:
                        S.op("act", lambda: A.activation(out=ys[i][:, 0:512], in_=pb[:, :], func=AF.Copy), reads=[bpb], writes=[b_ys[i]])
                    else:
                        S.op("dve", lambda: V.tensor_copy(out=ys[i][:, 512:1024], in_=pb[:, :]), reads=[bpb], writes=[b_ys[i]])
                S.dma("sp", y_d[c * 128:(c + 1) * 128, :], ys[i][:], ys_tls[i], reads=[b_ys[i]])
            S.final_wait("sp")
        assert wq["n"] == len(wq["plan"]), (wq["n"], len(wq["plan"]))
    return nc


_CACHE = {}


def _consts():
    if "c" in _CACHE:
        return _CACHE["c"]
    i = np.arange(128)
    trif = (i[:, None] <= i[None, :]).astype(np.float32)
    trib = (i[:, None] >= i[None, :]).astype(np.float32)
    cmat = np.concatenate([trif, trib, np.eye(128, dtype=np.float32), np.full((128, 128), 1.0 / 1024, np.float32)], axis=1)
    c256 = np.arange(256, dtype=np.float64)
    ang = 2 * np.pi * np.outer(c256, c256) / 256.0
    dcs = np.concatenate([np.cos(ang), np.sin(ang)], axis=1).astype(np.float32)

    def tmat(T, scale):
        t = np.arange(T, dtype=np.float64)
        a = 2 * np.pi * (np.outer(t, t) % T) / T
        return np.cos(a) * scale, -np.sin(a) * scale
    cs, ns = tmat(1024, 1.0 / np.sqrt(1024 * 256.0))
    tA_s = np.stack([cs, ns]).astype(np.float32)
    cp, npn = tmat(256, 1.0 / 256.0)
    tB = np.stack([cp, npn]).astype(np.float32)
    tA_p = np.zeros((2, 1024, 1024), np.float32)
    for s in range(4):
        tA_p[0, s * 256:(s + 1) * 256, s * 256:(s + 1) * 256] = cp
        tA_p[1, s * 256:(s + 1) * 256, s * 256:(s + 1) * 256] = npn
    quarter = D // 4
    freqs = (10000.0 ** (-np.arange(quarter, dtype=np.float32) / np.float32(quarter))).astype(np.float32)
    rows = 16
    r = np.repeat(np.arange(rows, dtype=np.float32), 64)[:, None] * freqs
    col = np.tile(np.arange(64, dtype=np.float32), rows)[:, None] * freqs
    pos = np.concatenate([np.sin(r), np.cos(r), np.sin(col), np.cos(col)], axis=-1).astype(np.float32)
    t = np.arange(NTOK)
    def masks(rowlenA):
        rl = np.where(t < 1024, rowlenA, 256)
        ml = ((t % rl) == 0).astype(np.float32)
        mr = ((t % rl) == rl - 1).astype(np.float32)
        return np.tile(np.concatenate([ml, mr])[None, :], (128, 1)).astype(np.float32)
    def keep(kA):
        k = np.zeros((12, 2, 4), np.float32)
        fw = [1, 1, kA, 1, kA, 1, kA, 1, 0, 1, 0, 1]
        bw = [1, kA, 1, kA, 1, kA, 1, 1, 1, 0, 1, 0]
        for c in range(12):
            k[c, 0, :] = fw[c]
            k[c, 1, :] = bw[c]
        return k.reshape(1, 96)
    c = dict(cmat=cmat, dcs=dcs, tA_s=tA_s, tA_p=tA_p, tB=tB, pos=pos, pos0=np.zeros_like(pos),
             cmask_s=masks(64), cmask_p=masks(256), keep_s=keep(1.0), keep_p=keep(0.0))
    _CACHE["c"] = c
    return c


def kernel(x_prompt, x_sample, state_C, state_n, state_m, c, c_ctx, w_ada, b_ada, w_in, b_in,
           mlstm_norm_g, conv_w, conv_b, w_br_mlstm, w_br_fourier, w_br_conv, w_out, ln_g, ln_b,
           w_gate_up, w_down):
    f32 = np.float32
    K = _consts()
    A = lambda a: np.ascontiguousarray(np.asarray(a, dtype=f32))
    x_prompt, x_sample = A(x_prompt), A(x_sample)
    state_C, state_n, state_m = A(state_C), A(state_n), A(state_m)
    c, c_ctx = A(c), A(c_ctx)
    b_ada, b_in = A(b_ada), A(b_in)
    Wd = {"w_ada": A(w_ada), "w_in": A(w_in), "w_br_mlstm": A(w_br_mlstm), "w_br_fourier": A(w_br_fourier),
          "w_br_conv": A(w_br_conv), "w_out": A(w_out), "w_gate_up": A(w_gate_up), "w_down": A(w_down)}
    wpack_s = pack_weights(Wd)
    wpack_p = wpack_s
    if "tAp" not in _CACHE:
        _CACHE["tAp"] = (pack_tA(K["tA_s"]), pack_tA(K["tA_p"]))
    tAp_s, tAp_p = _CACHE["tAp"]
    shared = {"cmat": K["cmat"], "dcs": K["dcs"], "tB": K["tB"]}
    badaT = np.repeat(b_ada.reshape(DEPTH, 48, 128).transpose(0, 2, 1)[:, :, :, None], 2, axis=3).reshape(DEPTH, 128, 96)
    colp = np.zeros((DEPTH, 128, NCOL), f32)
    for l in range(DEPTH):
        for name, bi in BLK.items():
            off = BLK_OFF[name]
            colp[l, :, bi * 8:bi * 8 + 8] = b_in[l, off:off + 1024].reshape(8, 128).T
        colp[l, :, C_MNG:C_MNG + 8] = A(mlstm_norm_g)[l].reshape(8, 128).T
        cw = A(conv_w)
        colp[l, :, C_W0:C_W0 + 8] = cw[l, 0].reshape(8, 128).T
        colp[l, :, C_W1:C_W1 + 8] = cw[l, 1].reshape(8, 128).T
        colp[l, :, C_W2:C_W2 + 8] = cw[l, 2].reshape(8, 128).T
        colp[l, :, C_CB:C_CB + 8] = A(conv_b)[l].reshape(8, 128).T
        colp[l, :, C_LNG0:C_LNG0 + 8] = A(ln_g)[l, 0].reshape(8, 128).T
        colp[l, :, C_LNB0:C_LNB0 + 8] = A(ln_b)[l, 0].reshape(8, 128).T
        colp[l, :, C_LNG1:C_LNG1 + 8] = A(ln_g)[l, 1].reshape(8, 128).T
        colp[l, :, C_LNB1:C_LNB1 + 8] = A(ln_b)[l, 1].reshape(8, 128).T
    bg_bc = np.ascontiguousarray(np.broadcast_to(np.tile(b_in[:, G_OFF:G_OFF + 16], (1, 12))[:, None, :], (DEPTH, 128, 192)))
    bkv = np.zeros((DEPTH, H, 128, 4, 256), f32)
    for l in range(DEPTH):
        for h in range(H):
            bkv[l, h, :, 0:2, :] = b_in[l, K_OFF + h * 256:K_OFF + (h + 1) * 256][None, None, :]
            bkv[l, h, :, 2:4, :] = b_in[l, V_OFF + h * 256:V_OFF + (h + 1) * 256][None, None, :]
    bkv = bkv.reshape(DEPTH, H, 128, 1024)
    shared.update(badaT=np.ascontiguousarray(badaT), colp=colp, bg_bc=bg_bc, bkv_bc=bkv)

    in_maps = []
    prompts_of = []
    for core in range(8):
        m = dict(shared)
        if core < 4:
            pa = None
            pb_ = [2 * core, 2 * core + 1]
            xa = x_sample[core]
            condA = c[core]
            m["pos"] = K["pos"]; m["wpack"] = wpack_s; m["tApack"] = tAp_s; m["cmask"] = K["cmask_s"]; m["keep"] = K["keep_s"]
            m["C0"] = np.ascontiguousarray(state_C[core])
            n0 = state_n[core]
            m["m0"] = np.ascontiguousarray(state_m[core].reshape(DEPTH, 1, 8))
        else:
            base = 8 + 6 * (core - 4)
            pa = [base, base + 1, base + 2, base + 3]
            pb_ = [base + 4, base + 5]
            xa = x_prompt[pa].reshape(1024, D)
            condA = c_ctx
            m["pos"] = K["pos0"]; m["wpack"] = wpack_p; m["tApack"] = tAp_p; m["cmask"] = K["cmask_p"]; m["keep"] = K["keep_p"]
            m["C0"] = np.zeros((DEPTH, 2, H, DH, DH), f32)
            n0 = np.zeros((DEPTH, 2, H, DH), f32)
            m["m0"] = np.zeros((DEPTH, 1, 8), f32)
        m["n0T"] = np.ascontiguousarray(n0.reshape(DEPTH, 2, H, 2, 128).transpose(0, 4, 3, 1, 2).reshape(DEPTH, 128, 2, 8))
        m["xin"] = np.ascontiguousarray(np.concatenate([xa, x_prompt[pb_].reshape(512, D)], axis=0))
        m["condT"] = np.ascontiguousarray(np.stack([condA, c_ctx], axis=1))
        in_maps.append(m)
        prompts_of.append((pa, pb_))

    if "nc" not in _CACHE:
        _CACHE["nc"] = build_program()
    res = run_bass_kernel_spmd(_CACHE["nc"], in_maps, core_ids=list(range(8)))

    y_prompt = np.zeros((32, 256, D), f32)
    y_sample = np.zeros((4, 1024, D), f32)
    nC = np.zeros((32, DEPTH, 2, H, DH, DH), f32)
    nn = np.zeros((32, DEPTH, 2, H, DH), f32)
    nm = np.zeros((32, DEPTH, 2, H), f32)
    for core in range(8):
        r = res.results[core]
        y = r["y"]; oC = r["oC"]; on = r["on_raw"].reshape(DEPTH, 128, 2, 48); om = r["om_raw"].reshape(DEPTH, 12, 2, 4)
        pa, pb_ = prompts_of[core]
        segs = []
        if pa is None:
            y_sample[core] = y[0:1024]
        else:
            for s, p in enumerate(pa):
                y_prompt[p] = y[s * 256:(s + 1) * 256]
                segs.append((s, p))
        for s, p in enumerate(pb_):
            y_prompt[p] = y[1024 + s * 256:1024 + (s + 1) * 256]
            segs.append((4 + s, p))
        for s, p in segs:
            nC[p] = oC[s]
            blk = on[:, :, :, s * 8:(s + 1) * 8].reshape(DEPTH, 128, 2, 2, H)
            nn[p] = blk.transpose(0, 3, 4, 2, 1).reshape(DEPTH, 2, H, DH)
            nm[p, :, 0, :] = om[:, 2 * s + 1, 0, :]
            nm[p, :, 1, :] = om[:, 2 * s, 1, :]
    return (y_prompt, y_sample, nC, nn, nm)
```
